# Optimizing a Trainium2 kernel written in Bass

```python
import jax, jax.numpy as jnp
from jax import lax
import numpy as np

D_MODEL = 1024
BATCH = 8
SEQ = 2048
DEPTH = 4
DEC_BATCH = 128
DEC_SEQ = 4
PAST_LEN = 16384
PAGE_SIZE = 128

N_MIXERS = 2
N_LRU_LAYERS = (DEPTH + 1) // 2
N_SG_LAYERS = DEPTH // 2
D_RNN = 3 * D_MODEL // 2
N_LRU_HEADS = 16
LRU_BLOCK = D_RNN // N_LRU_HEADS
CONV_W = 4
LRU_C = 8.0
CHUNK = 128
D_SG = 3 * D_MODEL // 2
N_SG_GROUPS = 16
SG_GROUP = D_SG // N_SG_GROUPS
D_FF = 4 * D_MODEL
EPS = 1e-6

kernel_name = "hawk_gmlp_interleaved_decoder_step"


def rmsnorm(x, g):
    xf = x.astype(jnp.float32)
    y = xf * lax.rsqrt(jnp.mean(xf * xf, axis=-1, keepdims=True) + EPS)
    return (y * g.astype(jnp.float32)).astype(x.dtype)


def layernorm(x, g, b):
    xf = x.astype(jnp.float32)
    mu = jnp.mean(xf, axis=-1, keepdims=True)
    xc = xf - mu
    var = jnp.mean(xc * xc, axis=-1, keepdims=True)
    y = xc * lax.rsqrt(var + EPS) * g.astype(jnp.float32) + b.astype(jnp.float32)
    return y.astype(x.dtype)


def ada_params(c, w, b):
    mod = jax.nn.silu(c) @ w + b
    return jnp.split(mod[:, None, :], 6, axis=-1)


def causal_conv(xb, buf, w, b):
    t = xb.shape[1]
    xp = jnp.concatenate([buf.astype(xb.dtype), xb], axis=1)
    y = b
    for k in range(CONV_W):
        y = y + xp[:, k:k + t] * w[k]
    return y, xp[:, -(CONV_W - 1):]


def _lin_comb(left, right):
    a1, b1 = left
    a2, b2 = right
    return a1 * a2, a2 * b1 + b2


def rglru_mixer(h, conv_buf, h0, w_in, conv_w, conv_b, wa, ba, wx, bx, lam, w_out):
    n, t, _ = h.shape
    gate_br, x_br = jnp.split(h @ w_in, 2, axis=-1)
    xc, new_buf = causal_conv(x_br, conv_buf, conv_w, conv_b)
    xh = xc.reshape(n, t, N_LRU_HEADS, LRU_BLOCK)
    r = jax.nn.sigmoid((jnp.einsum('bthi,hij->bthj', xh, wa) + ba).astype(jnp.float32)).reshape(n, t, D_RNN)
    ig = jax.nn.sigmoid((jnp.einsum('bthi,hij->bthj', xh, wx) + bx).astype(jnp.float32)).reshape(n, t, D_RNN)
    log_a = -LRU_C * r * jax.nn.softplus(-lam.astype(jnp.float32))
    a = jnp.exp(log_a)
    mult = jnp.sqrt(-jnp.expm1(2.0 * log_a))
    bt = mult * (ig * xc.astype(jnp.float32))
    bt = bt.at[:, 0].add(a[:, 0] * h0.astype(jnp.float32))
    _, hs = lax.associative_scan(_lin_comb, (a, bt), axis=1)
    y = hs.astype(h.dtype) * jax.nn.gelu(gate_br)
    return y @ w_out, new_buf, hs[:, -1].astype(h0.dtype)


def chunk_sgu_mixer(h, w_in, b_in, ln_g, ln_b, ws, bs, w_out):
    n, t, _ = h.shape
    u, v = jnp.split(jax.nn.gelu(h @ w_in + b_in), 2, axis=-1)
    v = layernorm(v, ln_g, ln_b)
    cl = min(t, CHUNK)
    mask = jnp.tril(jnp.ones((cl, cl), dtype=bool))
    w_sp = jnp.where(mask, ws[:, :cl, :cl], 0).astype(v.dtype)
    vc = v.reshape(n, t // cl, cl, N_SG_GROUPS, SG_GROUP)
    mixed = jnp.einsum('gts,bcsgd->bctgd', w_sp, vc) + bs[:, :cl].T[None, None, :, :, None]
    out = u * mixed.reshape(n, t, D_SG)
    return out @ w_out, v


def run_trunk(x, c, conv_state, h_state, ada_w, ada_b, norm_g, lru_w_in, lru_conv_w, lru_conv_b,
              lru_wa, lru_ba, lru_wx, lru_bx, lru_lambda, lru_w_out, sg_w_in, sg_b_in, sg_ln_g,
              sg_ln_b, sg_ws, sg_bs, sg_w_out, mlp_w_up, mlp_w_down):
    convs, hs, vs = [], [], []
    for layer in range(DEPTH):
        sh_m, sc_m, g_m, sh_f, sc_f, g_f = ada_params(c, ada_w[layer], ada_b[layer])
        hin = rmsnorm(x, norm_g[layer, 0]) * (1.0 + sc_m) + sh_m
        j = layer // N_MIXERS
        if layer % N_MIXERS == 0:
            mix, buf, hn = rglru_mixer(hin, conv_state[j], h_state[j], lru_w_in[j], lru_conv_w[j],
                                       lru_conv_b[j], lru_wa[j], lru_ba[j], lru_wx[j], lru_bx[j],
                                       lru_lambda[j], lru_w_out[j])
            convs.append(buf)
            hs.append(hn)
        else:
            mix, v = chunk_sgu_mixer(hin, sg_w_in[j], sg_b_in[j], sg_ln_g[j], sg_ln_b[j],
                                     sg_ws[j], sg_bs[j], sg_w_out[j])
            vs.append(v)
        x = x + g_m * rmsnorm(mix, norm_g[layer, 1])
        hf = rmsnorm(x, norm_g[layer, 2]) * (1.0 + sc_f) + sh_f
        f = jnp.square(jax.nn.relu(hf @ mlp_w_up[layer])) @ mlp_w_down[layer]
        x = x + g_f * rmsnorm(f, norm_g[layer, 3])
    return x, jnp.stack(convs), jnp.stack(hs), jnp.stack(vs)


def setup_inputs(seed: int = 0) -> dict:
    key = jax.random.key(seed)
    ks = iter(jax.random.split(key, 40))
    f32 = jnp.float32

    def nrm(shape, s):
        return jax.random.normal(next(ks), shape, f32) * s

    D = D_MODEL
    NL, NC, H, BS, G = N_LRU_LAYERS, N_SG_LAYERS, N_LRU_HEADS, LRU_BLOCK, N_SG_GROUPS
    a0 = jax.random.uniform(next(ks), (NL, D_RNN), f32, 0.9, 0.999)
    p = a0 ** (1.0 / LRU_C)
    return {
        "x_prompt": nrm((BATCH, SEQ, D), 1.0),
        "x_sample": nrm((DEC_BATCH, DEC_SEQ, D), 1.0),
        "c_prompt": nrm((BATCH, D), 1.0),
        "c_sample": nrm((DEC_BATCH, D), 1.0),
        "state_lru_h": nrm((NL, DEC_BATCH, D_RNN), 0.5),
        "state_lru_conv": nrm((NL, DEC_BATCH, CONV_W - 1, D_RNN), 1.0),
        "ada_w": nrm((DEPTH, D, 6 * D), 0.5 * D ** -0.5),
        "ada_b": nrm((DEPTH, 6 * D), 0.02),
        "norm_g": 1.0 + nrm((DEPTH, 4, D), 0.05),
        "lru_w_in": nrm((NL, D, 2 * D_RNN), D ** -0.5),
        "lru_conv_w": nrm((NL, CONV_W, D_RNN), CONV_W ** -0.5),
        "lru_conv_b": nrm((NL, D_RNN), 0.02),
        "lru_wa": nrm((NL, H, BS, BS), BS ** -0.5),
        "lru_ba": nrm((NL, H, BS), 0.02),
        "lru_wx": nrm((NL, H, BS, BS), BS ** -0.5),
        "lru_bx": nrm((NL, H, BS), 0.02),
        "lru_lambda": jnp.log(p) - jnp.log1p(-p),
        "lru_w_out": nrm((NL, D_RNN, D), D_RNN ** -0.5),
        "sg_w_in": nrm((NC, D, 2 * D_SG), D ** -0.5),
        "sg_b_in": nrm((NC, 2 * D_SG), 0.02),
        "sg_ln_g": 1.0 + nrm((NC, D_SG), 0.05),
        "sg_ln_b": nrm((NC, D_SG), 0.02),
        "sg_ws": nrm((NC, G, CHUNK, CHUNK), CHUNK ** -0.5),
        "sg_bs": 1.0 + nrm((NC, G, CHUNK), 0.1),
        "sg_w_out": nrm((NC, D_SG, D), D_SG ** -0.5),
        "mlp_w_up": nrm((DEPTH, D, D_FF), D ** -0.5),
        "mlp_w_down": nrm((DEPTH, D_FF, D), D_FF ** -0.5),
    }


def reference(x_prompt, x_sample, c_prompt, c_sample, state_lru_h, state_lru_conv, ada_w, ada_b,
              norm_g, lru_w_in, lru_conv_w, lru_conv_b, lru_wa, lru_ba, lru_wx, lru_bx, lru_lambda,
              lru_w_out, sg_w_in, sg_b_in, sg_ln_g, sg_ln_b, sg_ws, sg_bs, sg_w_out, mlp_w_up,
              mlp_w_down):
    b = x_prompt.shape[0]
    zero_conv = jnp.zeros((N_LRU_LAYERS, b, CONV_W - 1, D_RNN), x_prompt.dtype)
    zero_h = jnp.zeros((N_LRU_LAYERS, b, D_RNN), state_lru_h.dtype)
    y_prompt, conv_prompt, h_prompt, _ = run_trunk(
        x_prompt, c_prompt, zero_conv, zero_h, ada_w, ada_b, norm_g, lru_w_in, lru_conv_w,
        lru_conv_b, lru_wa, lru_ba, lru_wx, lru_bx, lru_lambda, lru_w_out, sg_w_in, sg_b_in,
        sg_ln_g, sg_ln_b, sg_ws, sg_bs, sg_w_out, mlp_w_up, mlp_w_down)
    y_sample, conv_sample, h_sample, v_sample = run_trunk(
        x_sample, c_sample, state_lru_conv, state_lru_h, ada_w, ada_b, norm_g, lru_w_in, lru_conv_w,
        lru_conv_b, lru_wa, lru_ba, lru_wx, lru_bx, lru_lambda, lru_w_out, sg_w_in, sg_b_in,
        sg_ln_g, sg_ln_b, sg_ws, sg_bs, sg_w_out, mlp_w_up, mlp_w_down)
    return (y_prompt, y_sample, h_prompt, conv_prompt, h_sample, conv_sample, v_sample)
```

```python
import numpy as np
import concourse.bass as bass
import concourse.mybir as mybir
from concourse.bass_utils import run_bass_kernel_spmd

F32 = mybir.dt.float32
BF16 = mybir.dt.bfloat16
F32R = mybir.dt.float32r
AF = mybir.ActivationFunctionType
ALU = mybir.AluOpType

D = 1024
DR = 1536
NH = 16
HB = 96
DFF = 4096
EPS = 1e-6
LN_HALF = -0.6931471805599453
NCORE = 8
SEQ = 2048
NS = 16
TS = 4
NSMP = NS * TS

GROUPS = [[("p", 0)], [("p", 512)], [("p", 1024)], [("p", 1536), ("s", 0)]]
NG = 576


class Buf:
    __slots__ = ("w", "r", "name", "dsem", "dcnt")

    def __init__(self, name):
        self.w = None
        self.r = {}
        self.name = name
        self.dsem = None
        self.dcnt = 0


class Eng:
    def __init__(self, name, sem, is_pe=False):
        self.name = name
        self.sem = sem
        self.n = 0
        self.seen = {}
        self.is_pe = is_pe
        self.prog = []


class Rec:
    def __init__(self):
        self.calls = []

    def __getattr__(self, name):
        def f(*a, **kw):
            self.calls.append((name, a, kw))
            return len(self.calls) - 1
        return f


def build_nc():
    nc = bass.Bass("TRN2", target_bir_lowering=False)

    def din(name, shape):
        return nc.dram_tensor(name, list(shape), F32, kind="ExternalInput").ap()

    def dout(name, shape):
        return nc.dram_tensor(name, list(shape), F32, kind="ExternalOutput").ap()

    xp = din("xp", [SEQ, D]); xs = din("xs", [NSMP, D]); c17 = din("c17", [17, D])
    h0s = din("h0s", [2, NS, DR]); cvs = din("cvs", [2, 3 * NS, DR])
    ada_w = din("ada_w", [4, D, 6 * D]); ada_b = din("ada_b", [4, 6 * D]); norm_g = din("norm_g", [16, D])
    lru_w_in = din("lru_w_in", [2, D, 2 * DR]); lru_conv_w = din("lru_conv_w", [2, 4, DR])
    lru_conv_b = din("lru_conv_b", [2, DR]); lru_wa = din("lru_wa", [2, NH, HB, HB]); lru_ba = din("lru_ba", [2, DR])
    lru_wx = din("lru_wx", [2, NH, HB, HB]); lru_bx = din("lru_bx", [2, DR]); lru_lam = din("lru_lambda", [2, DR])
    lru_w_out = din("lru_w_out", [2, DR, D]); sg_w_in = din("sg_w_in", [2, D, 2 * DR]); sg_b_in = din("sg_b_in", [2, 2 * DR])
    sg_ln_g = din("sg_ln_g", [2, DR]); sg_ln_b = din("sg_ln_b", [2, DR]); sg_ws = din("sg_ws", [2, NH, 128, 128])
    sg_bs = din("sg_bs", [2, NH, 128]); sg_w_out = din("sg_w_out", [2, DR, D])
    mlp_w_up = din("mlp_w_up", [4, D, DFF]); mlp_w_down = din("mlp_w_down", [4, DFF, D])
    cst = din("cst", [128, 384])

    yp = dout("yp", [SEQ, D]); ys = dout("ys", [NSMP, D]); hp_o = dout("hp", [2, DR]); cp_o = dout("cp", [2, 3, DR])
    hs_o = dout("hs", [2, NS, DR]); cs_o = dout("cs", [2, 3 * NS, DR]); vs_o = dout("vs", [2, NSMP, DR])

    from contextlib import ExitStack
    es = ExitStack()

    def sb(name, shape, dt):
        return es.enter_context(nc.sbuf_tensor(name, list(shape), dt))

    def sem(name):
        return es.enter_context(nc.semaphore(name))

    xg = sb("xg", [128, 8, NG], F32)
    RH = sb("RH", [128, 8 * NG], F32)
    hin = RH[:, 0:4 * NG].bitcast(BF16).rearrange("p (k n) -> p k n", k=8)
    fstore = RH[:, :].rearrange("p (k n) -> p k n", k=8)
    NBIG = max(32 * NG, 16 * NG + 8 * DR)
    BIG = sb("BIG", [128, NBIG], BF16)
    hidden = BIG[:, 0:32 * NG].rearrange("p (h n) -> p h n", h=32)
    ybuf = BIG[:, 0:16 * NG].rearrange("p (h n) -> p h n", h=16)
    wres = BIG[:, 16 * NG:16 * NG + 8 * DR]
    wv = wres[:, 0:8 * DR].rearrange("p (k n) -> p k n", k=8)
    SCR = sb("SCR", [128, 8448], F32)
    def ltmp(i):
        return SCR[0:HB, i * 520:(i + 1) * 520]
    vbf = SCR[:, 0:3840].bitcast(BF16).rearrange("p (c n) -> p c n", c=5)
    vg = SCR[:, 3840:5376]
    lngb = SCR[:, 5376:6912]
    lnbb = SCR[:, 6912:8448]
    NSTG = 4
    stg = [sb("stg%d" % i, [128, 1024], F32) for i in range(NSTG)]
    NWS = 6
    wsl = [sb("wsl%d" % i, [128, 1024], BF16) for i in range(NWS)]
    ada_fm = sb("ada_fm", [128, 4, 48, 17], F32)
    cst_t = sb("cst_t", [128, 384], F32)
    ident = cst_t[:, 0:128]; trilm = cst_t[:, 128:256]; Dm = cst_t[0:64, 256:320]; Pm = cst_t[0:4, 320:384]
    ones_r = sb("ones_r", [128, 128], F32R)
    ones_f = sb("ones_f", [128, 128], F32)
    ones_b = sb("ones_b", [128, 128], BF16)
    scT = sb("scT", [128, 8, 17], F32)
    ngT = sb("ngT", [128, 8, 16], F32)
    sqt = [sb("sqt%d" % i, [128, 512], F32R) for i in range(3)]
    rstd = sb("rstd", [128, 512], F32)
    tmpa = [sb("tmpa%d" % i, [128, 512], F32) for i in range(2)]
    lvec_in = sb("lvec_in", [128, HB], F32)
    lvec = sb("lvec", [HB, 128], F32)
    gw = sb("gw", [HB, 2, NH, HB], BF16)
    carry_h = sb("carry_h", [HB, 2, NH, 1], F32)
    carry_c = sb("carry_c", [HB, 2, 3, NH], F32)
    hsm = sb("hsm", [HB, 2, NH, NS], F32)
    csm = sb("csm", [HB, 2, NH, 3 * NS], F32)
    sgv_in = sb("sgv_in", [16, HB], F32)
    sgbu = sb("sgbu", [HB, NH], F32)
    e4 = sb("e4", [4, NH, 4], F32)
    x4 = sb("x4", [4, NH, 4, NS], F32)
    SG2 = sb("SG2", [128, 3848], F32)
    wtg = SG2[:, 0:1024].bitcast(BF16).rearrange("p (g t) -> p g t", g=NH)
    wts = SG2[0:64, 1024:1536].bitcast(BF16).rearrange("p (g t) -> p g t", g=NH)
    bsr = SG2[0:1, 1536:2560].bitcast(BF16).rearrange("p (g t) -> p g t", g=NH)
    bss = SG2[0:1, 2560:3072].bitcast(BF16).rearrange("p (g t s) -> p g t s", g=NH, t=4)
    bvrow = SG2[0:1, 3072:3840].bitcast(BF16)
    lnst = sb("lnst", [128, 3, 6], F32)
    lnag = sb("lnag", [128, 2], F32)
    lnr = sb("lnr", [128, 2], F32)
    xtok = sb("xtok", [128, D], F32)
    st_in = RH[0:64, 4 * NG:4 * NG + DR]
    c17_t = xtok[0:17, 0:D]
    ng_t = SCR[0:16, 0:D]
    modT = xtok[0:17, 0:256]

    ps = [es.enter_context(nc.psum_tensor("ps%d" % i, [128, 512], F32)) for i in range(8)]

    PE = Eng("tensor", sem("s_pe"), True)
    ACT = Eng("scalar", sem("s_act"))
    DVE = Eng("vector", sem("s_dve"))
    POOL = Eng("gpsimd", sem("s_pool"))
    SP = Eng("sync", sem("s_sp"))
    dsems = []

    def new_dbuf(name):
        b = Buf(name)
        b.dsem = sem("d_" + name)
        dsems.append(b)
        return b

    def _waits(E, reads, writes):
        need = {}

        def add(ev):
            if ev is None:
                return
            s, v = ev
            k = id(s)
            if k not in need or need[k][1] < v:
                need[k] = (s, v)
        for b in reads:
            add(b.w)
        for b in writes:
            add(b.w)
            for ev in b.r.values():
                add(ev)
        for k, (s, v) in need.items():
            if E.seen.get(k, 0) >= v:
                continue
            if s is E.sem and E.is_pe:
                continue
            E.prog.append(("wait", s, v))
            E.seen[k] = v

    def _record(ev, reads, writes):
        k = id(ev[0])
        for b in reads:
            if k not in b.r or b.r[k][1] < ev[1]:
                b.r[k] = ev
        for b in writes:
            b.w = ev
            b.r = {}

    def inherit(dst_bufs, src_bufs):
        for d_ in dst_bufs:
            for s_ in src_bufs:
                for ev in ([s_.w] if s_.w else []) + list(s_.r.values()):
                    k = id(ev[0])
                    if k not in d_.r or d_.r[k][1] < ev[1]:
                        d_.r[k] = ev

    def emit(E, fn, reads=(), writes=()):
        _waits(E, reads, writes)
        rec = Rec()
        fn(rec)
        E.n += 1
        E.prog.append(("ins", rec.calls))
        ev = (E.sem, E.n)
        _record(ev, reads, writes)
        return ev

    def dma(out, in_, db, reads=(), writes=(), E=None, **kw):
        E = E or SP
        _waits(E, reads, writes)
        E.prog.append(("dma", out, in_, kw, db.dsem))
        db.dcnt += 1
        ev = (db.dsem, 16 * db.dcnt)
        _record(ev, reads, writes)
        return ev

    B = {}

    def bf(name):
        if name not in B:
            B[name] = Buf(name)
        return B[name]

    b_ps = [bf("ps%d" % i) for i in range(8)]
    b_stg = [new_dbuf("stg%d" % i) for i in range(NSTG)]
    b_wsl = [bf("wsl%d" % i) for i in range(NWS)]
    cnt = {"stg": 0, "wsl": 0, "mm": 0, "cast": 0, "sq": 0, "ta": 0, "tr": 0, "ua": 0}
    b_cst = new_dbuf("cst"); b_xtok = new_dbuf("xtok"); b_small = b_xtok; b_stin = new_dbuf("stin"); b_ngt = new_dbuf("ngt"); b_xalt = new_dbuf("xalt"); b_xalt2 = new_dbuf("xalt2"); b_xtok2 = new_dbuf("xtok2")
    b_hid = bf("hidden"); b_fst = bf("fstore")
    b_out = new_dbuf("outs")

    def next_mm():
        i = (0, 1, 2, 3, 4, 6)[cnt["mm"] % 6]
        cnt["mm"] += 1
        return ps[i], b_ps[i]

    def next_tr():
        i = (7, 5)[cnt["tr"] % 2]
        cnt["tr"] += 1
        return ps[i], b_ps[i]

    cast_cfg = {"engs": [ACT, DVE, ACT, DVE, POOL, ACT, DVE]}

    def cast(out, in_, reads, writes):
        ce = cast_cfg["engs"]
        E = ce[cnt["cast"] % len(ce)]
        cnt["cast"] += 1
        if E is ACT:
            return emit(E, lambda e: e.activation(out=out, in_=in_, func=AF.Copy), reads, writes)
        return emit(E, lambda e: e.tensor_copy(out=out, in_=in_), reads, writes)

    def load_cast(dst, dst_buf, srcs, np_=128):
        i = cnt["stg"] % NSTG
        cnt["stg"] += 1
        st, sbuf_ = stg[i], b_stg[i]
        views = []
        for (vf, src) in srcs:
            v = vf(st)
            dma(v, src, sbuf_, writes=[sbuf_])
            views.append(v)
        return st, sbuf_

    def act(fn, reads, writes):
        return emit(ACT, fn, reads, writes)

    def dve(fn, reads, writes):
        return emit(DVE, fn, reads, writes)

    def pool(fn, reads, writes):
        return emit(POOL, fn, reads, writes)

    def pe(fn, reads, writes):
        return emit(PE, fn, reads, writes)

    def transpose_to(dst_ap, dst_buf, src_ap, src_buf, rows, cols, eng="act"):
        dsts = [dst_buf, b_fst] if dst_buf is b_stin else [dst_buf]
        return _transpose_to(dst_ap, dsts, src_ap, src_buf, rows, cols, eng)

    def _transpose_to(dst_ap, dst_bufs, src_ap, src_buf, rows, cols, eng="act"):
        pt, bpt = next_tr()
        srcs = list(src_buf) if isinstance(src_buf, (list, tuple)) else [src_buf]
        pe(lambda e: e.transpose(pt[0:cols, 0:rows], src_ap, ident[0:rows, 0:rows]), srcs + [b_cst], [bpt])
        if eng == "act":
            act(lambda e: e.activation(out=dst_ap, in_=pt[0:cols, 0:rows], func=AF.Copy), [bpt], dst_bufs)
        else:
            dve(lambda e: e.tensor_copy(out=dst_ap, in_=pt[0:cols, 0:rows]), [bpt], dst_bufs)

    def next_wsl():
        i = cnt["wsl"] % NWS
        cnt["wsl"] += 1
        return wsl[i], b_wsl[i]

    def next_stg():
        i = cnt["stg"] % NSTG
        cnt["stg"] += 1
        return stg[i], b_stg[i]

    def load_unit(src_ap, np_, shape_str, **dims):
        st, sbf = next_stg()
        nel = 1
        for d_ in src_ap.shape[1:]:
            nel *= d_
        stv = st[0:np_, 0:nel].rearrange(shape_str, **dims)
        dma(stv, src_ap, sbf, writes=[sbf])
        wu, bwu = next_wsl()
        wuv = wu[0:np_, 0:nel].rearrange(shape_str, **dims)
        cast(wuv, stv, [sbf], [bwu])
        return wuv, bwu

    def stream(loaders, consume, PF=5):
        hd = {}
        nU = len(loaders)
        for i in range(nU + PF):
            if i < nU:
                hd[i] = loaders[i]()
            if i >= PF:
                consume(i - PF, hd.pop(i - PF))

    b_c = bf("consts")
    dma(cst_t[:], cst, b_cst, writes=[b_cst])
    dve(lambda e: e.memset(ones_f[:], 1.0), [], [b_c])
    dve(lambda e: e.tensor_copy(out=ones_r[:], in_=ones_f[:]), [b_c], [b_c])
    dve(lambda e: e.memset(ones_b[:], 1.0), [], [b_c])
    dve(lambda e: e.memset(carry_h[:], 0.0), [], [bf("carry")])
    ev_c = dve(lambda e: e.memset(carry_c[:], 0.0), [], [bf("carry")])
    for j_ in range(2):
        for h_ in range(NH):
            bf("carry_%d_%d" % (j_, h_)).w = ev_c

    b_scT = bf("scT"); b_ada = bf("ada"); b_modT = bf("modT"); b_ngT = bf("ngT")
    dma(c17_t, c17, b_small, writes=[b_small])
    act(lambda e: e.activation(out=c17_t, in_=c17_t, func=AF.Silu), [b_small], [b_small])
    for k in range(8):
        transpose_to(scT[:, k, :], b_scT, c17_t[0:17, k * 128:(k + 1) * 128], b_small, 17, 128)
    dma(ng_t, norm_g, b_ngt, writes=[b_ngt])
    for k in range(8):
        transpose_to(ngT[:, k, :], b_ngT, ng_t[0:16, k * 128:(k + 1) * 128], b_ngt, 16, 128)
    brs = [(xtok[0:1, 256 + 128 * i_:384 + 128 * i_], new_dbuf("brow%d" % i_)) for i_ in range(4)]
    ada_xt = [bf("modT0"), bf("modT1")] + [b_ for _, b_ in brs]
    inherit(ada_xt, [b_xtok])

    def ada_unit(l, ct):
        awv = ada_w[l].rearrange("(k p) n -> p k n", p=128)
        st, sbf = next_stg()
        stv = st[:, :].rearrange("p (k n) -> p k n", k=8)
        dma(stv, awv[:, :, ct * 128:(ct + 1) * 128], sbf, writes=[sbf])
        brow_, b_brow_ = brs[ct % 4]
        dma(brow_, ada_b[l:l + 1, ct * 128:(ct + 1) * 128], b_brow_, writes=[b_brow_])
        pm, bpm = next_tr()

        def mm_ada(e):
            for k in range(8):
                e.matmul(pm[0:17, 0:128], scT[:, k, :], stv[:, k, :], start=(k == 0), stop=False)
            return e.matmul(pm[0:17, 0:128], ones_f[0:1, 0:17], brow_, start=False, stop=True)
        pe(mm_ada, [sbf, b_brow_, b_scT, b_c], [bpm])
        mt = modT[:, (ct % 2) * 128:(ct % 2 + 1) * 128]
        bmt = bf("modT%d" % (ct % 2))
        act(lambda e: e.activation(out=mt, in_=pm[0:17, 0:128], func=AF.Copy), [bpm], [bmt])
        transpose_to(ada_fm[:, l, ct, :], bf("ada%d" % l), mt, bmt, 17, 128, eng="dve")

    def ada_finish(l):
        b_al = bf("ada%d" % l)
        for blk, gi_, plus1 in ((1, 0, True), (2, 1, False), (4, 2, True), (5, 3, False)):
            a_ = ada_fm[:, l, blk * 8:(blk + 1) * 8, :]
            g_ = ngT[:, :, l * 4 + gi_:l * 4 + gi_ + 1].to_broadcast([128, 8, 17])
            if plus1:
                dve(lambda e, a_=a_, g_=g_: e.scalar_tensor_tensor(out=a_, in0=a_, scalar=1.0, in1=g_, op0=ALU.add,
                                                                  op1=ALU.mult), [b_al, b_ngT], [b_al])
            else:
                dve(lambda e, a_=a_, g_=g_: e.tensor_tensor(out=a_, in0=a_, in1=g_, op=ALU.mult), [b_al, b_ngT], [b_al])

    for ct in range(48):
        ada_unit(0, ct)
    ada_finish(0)
    ada_pending = {"l": 1, "ct": 0}

    def ada_spread(k):
        for _ in range(k):
            l_ = ada_pending["l"]
            if l_ > 3:
                return
            ada_unit(l_, ada_pending["ct"])
            ada_pending["ct"] += 1
            if ada_pending["ct"] == 48:
                ada_finish(l_)
                ada_pending["l"] += 1
                ada_pending["ct"] = 0

    def modp(l, blk, k):
        return ada_fm[:, l, blk * 8 + k, 0:1]

    def mods(l, blk, k):
        return ada_fm[:, l, blk * 8 + k, 1:17].unsqueeze(1).to_broadcast([128, TS, NS])

    def v3(ap):
        return ap.rearrange("p (t s) -> p t s", t=TS)

    class Tile:
        pass

    def stats_rstd(src_fn, src_bufs, n):
        pst, bst = ps[5], b_ps[5]
        for k in range(8):
            i = cnt["sq"] % 3
            cnt["sq"] += 1
            sq, bsq = sqt[i], bf("sqt%d" % i)
            if k % 3 == 1:
                dve(lambda e, sq=sq, k=k: e.tensor_tensor(out=sq[:, 0:n], in0=src_fn(k), in1=src_fn(k), op=ALU.mult), src_bufs, [bsq])
            else:
                act(lambda e, sq=sq, k=k: e.activation(out=sq[:, 0:n], in_=src_fn(k), func=AF.Square), src_bufs, [bsq])
            pe(lambda e, sq=sq, k=k: e.matmul(pst[:, 0:n], ones_r[:], sq[:, 0:n], start=(k == 0), stop=(k == 7)),
               [bsq, b_c], [bst])
        b_rs = bf("rstd")
        act(lambda e: e.activation(out=rstd[:, 0:n], in_=pst[:, 0:n], func=AF.Ln, scale=1.0 / D, bias=EPS), [bst], [b_rs])
        act(lambda e: e.activation(out=rstd[:, 0:n], in_=rstd[:, 0:n], func=AF.Exp, scale=-0.5), [b_rs], [b_rs])
        return b_rs

    def pre_norm(l, tl, blk_g, blk_sh, b_dst):
        n, c0 = tl.n, tl.c0
        b_rs = stats_rstd(lambda k: xg[:, k, c0:c0 + n], [tl.bx], n)
        for k in range(8):
            i = cnt["ta"] % 2
            cnt["ta"] += 1
            ta, bta = tmpa[i], bf("tmpa%d" % i)
            xin = xg[:, k, c0:c0 + n]
            dst = hin[:, k, c0:c0 + n]
            if tl.kind == "p":
                dve(lambda e, ta=ta, xin=xin, k=k: e.scalar_tensor_tensor(out=ta[:, 0:n], in0=xin, scalar=modp(l, blk_g, k),
                                                                      in1=rstd[:, 0:n], op0=ALU.mult, op1=ALU.mult),
                    [tl.bx, b_rs, bf("ada%d" % l)], [bta])
                act(lambda e, ta=ta, dst=dst, k=k: e.activation(out=dst, in_=ta[:, 0:n], func=AF.Identity,
                                                              bias=modp(l, blk_sh, k), scale=1.0), [bta, bf("ada%d" % l)], b_dst)
            else:
                dve(lambda e, ta=ta, xin=xin: e.tensor_tensor(out=ta[:, 0:n], in0=xin, in1=rstd[:, 0:n], op=ALU.mult),
                    [tl.bx, b_rs], [bta])
                dve(lambda e, ta=ta, k=k: e.tensor_tensor(out=v3(ta[:, 0:n]), in0=v3(ta[:, 0:n]), in1=mods(l, blk_g, k),
                                                        op=ALU.mult), [bta, bf("ada%d" % l)], [bta])
                dve(lambda e, ta=ta, dst=dst, k=k: e.tensor_tensor(out=v3(dst), in0=v3(ta[:, 0:n]), in1=mods(l, blk_sh, k),
                                                                 op=ALU.add), [bta, bf("ada%d" % l)], b_dst)

    def post_norm_residual(l, tl, src_fn, src_bufs, blk):
        n, c0 = tl.n, tl.c0
        b_rs = stats_rstd(src_fn, src_bufs, n)
        for k in range(8):
            i = cnt["ta"] % 2
            cnt["ta"] += 1
            ta, bta = tmpa[i], bf("tmpa%d" % i)
            xin = xg[:, k, c0:c0 + n]
            if tl.kind == "p":
                dve(lambda e, ta=ta, k=k: e.scalar_tensor_tensor(out=ta[:, 0:n], in0=src_fn(k), scalar=modp(l, blk, k),
                                                              in1=rstd[:, 0:n], op0=ALU.mult, op1=ALU.mult),
                    list(src_bufs) + [b_rs, bf("ada%d" % l)], [bta])
            else:
                dve(lambda e, ta=ta, k=k: e.tensor_tensor(out=ta[:, 0:n], in0=src_fn(k), in1=rstd[:, 0:n], op=ALU.mult),
                    list(src_bufs) + [b_rs], [bta])
                dve(lambda e, ta=ta, k=k: e.tensor_tensor(out=v3(ta[:, 0:n]), in0=v3(ta[:, 0:n]), in1=mods(l, blk, k),
                                                        op=ALU.mult), [bta, bf("ada%d" % l)], [bta])
            dve(lambda e, ta=ta, xin=xin: e.tensor_tensor(out=xin, in0=xin, in1=ta[:, 0:n], op=ALU.add), [bta, tl.bx], [tl.bx])

    b_lvec = bf("lvec"); b_gw = bf("gw"); b_lvin = new_dbuf("lvin"); b_gwst = new_dbuf("gwst")

    def prep_lru(j):
        vecs = [lru_conv_w[j, 0], lru_conv_w[j, 1], lru_conv_w[j, 2], lru_conv_w[j, 3], lru_conv_b[j], lru_ba[j],
                lru_bx[j], lru_lam[j]]
        for vi, v in enumerate(vecs):
            dma(lvec_in[vi * 16:(vi + 1) * 16, :], v.rearrange("(h c) -> h c", c=HB), b_lvin, writes=[b_lvin])
        transpose_to(lvec[:, :], b_lvec, lvec_in[:, :], b_lvin, 128, HB)
        lam_ = lvec[:, 112:128]
        act(lambda e: e.activation(out=lam_, in_=lam_, func=AF.Exp, scale=-1.0), [b_lvec], [b_lvec])
        act(lambda e: e.activation(out=lam_, in_=lam_, func=AF.Ln, bias=1.0, scale=1.0), [b_lvec], [b_lvec])
        act(lambda e: e.mul(out=lam_, in_=lam_, mul=-4.0), [b_lvec], [b_lvec])
        act(lambda e: e.mul(out=lvec[:, 80:112], in_=lvec[:, 80:112], mul=0.5), [b_lvec], [b_lvec])
        for gi_, wsrc in enumerate((lru_wa, lru_wx)):
            for hh_ in range(2):
                st, sbf = next_stg()
                stv = st[0:HB, 0:8 * HB].rearrange("p (h j) -> p h j", h=8)
                dma(stv, wsrc[j, hh_ * 8:(hh_ + 1) * 8].rearrange("h i j -> i h j"), sbf, writes=[sbf])
                dve(lambda e, gi_=gi_, stv=stv, hh_=hh_: e.tensor_copy(out=gw[:, gi_, hh_ * 8:(hh_ + 1) * 8, :], in_=stv),
                    [sbf], [b_gw])

    def lv(vi, h):
        return lvec[:, vi * 16 + h:vi * 16 + h + 1]

    b_sgp = bf("sgp"); b_sgin = new_dbuf("sgin"); b_wsst = new_dbuf("wsst"); b_wt = bf("wt"); b_lnbc = new_dbuf("lnbc"); b_e4 = new_dbuf("e4")

    def prep_sgu(j, need_sample):
        dma(sgv_in[:, :], sg_b_in[j, 0:DR].rearrange("(g c) -> g c", c=HB), b_sgin, writes=[b_sgin])
        transpose_to(sgbu[:, :], b_sgp, sgv_in[:, :], b_sgin, 16, HB)
        for hh_ in range(2):
            st, sbf = next_stg()
            bvrow_f = st[0:1, 0:768]
            dma(bvrow_f, sg_b_in[j:j + 1, DR + hh_ * 768:DR + (hh_ + 1) * 768], sbf, writes=[sbf])
            dve(lambda e, hh_=hh_, bvrow_f=bvrow_f: e.tensor_copy(out=bvrow[0:1, hh_ * 768:(hh_ + 1) * 768], in_=bvrow_f),
                [sbf], [b_sgp])
            st, sbf = next_stg()
            bsr_f = st[0:1, 0:8 * 128].rearrange("p (g t) -> p g t", g=8)
            dma(bsr_f, sg_bs[j:j + 1, hh_ * 8:(hh_ + 1) * 8], sbf, writes=[sbf])
            dve(lambda e, hh_=hh_, bsr_f=bsr_f: e.tensor_copy(out=bsr[0:1, hh_ * 8:(hh_ + 1) * 8, :], in_=bsr_f), [sbf], [b_sgp])
            if need_sample:
                dve(lambda e, hh_=hh_, bsr_f=bsr_f: e.tensor_copy(
                    out=bss[0:1, hh_ * 8:(hh_ + 1) * 8],
                    in_=bsr_f[0:1, :, 0:TS].unsqueeze(3).to_broadcast([1, 8, TS, NS])), [sbf], [b_sgp])
        for q in range(4):
            st_, sbf_ = next_stg()
            wsq = st_[:, 0:512].rearrange("p (g s) -> p g s", g=4)
            dma(wsq, sg_ws[j, q * 4:(q + 1) * 4].rearrange("g t s -> t g s"), sbf_, writes=[sbf_])
            dve(lambda e, wsq=wsq: e.tensor_tensor(out=wsq, in0=wsq, in1=trilm.unsqueeze(1).to_broadcast([128, 4, 128]),
                                                  op=ALU.mult), [sbf_, b_cst], [sbf_])
            pt, bpt = next_tr()

            def trs(e, pt=pt, wsq=wsq):
                for gg in range(4):
                    ins = e.transpose(pt[:, gg * 128:(gg + 1) * 128], wsq[:, gg, :], ident[:, :])
                return ins
            pe(trs, [sbf_, b_cst], [bpt])
            act(lambda e, pt=pt, q=q: e.activation(out=wtg[:, q * 4:(q + 1) * 4, :],
                                                  in_=pt[:, :].rearrange("p (g t) -> p g t", g=4), func=AF.Copy),
                [bpt], [b_wt])
        if need_sample:
            for t_ in range(TS):
                dma(e4[:, :, t_], sg_ws[j, :, t_, 0:TS].rearrange("g u -> u g"), b_e4, writes=[b_e4],
                    allow_slow_non_contiguous=True)
            dve(lambda e: e.tensor_copy(out=x4[:], in_=e4[:, :, :].unsqueeze(3).to_broadcast([4, NH, TS, NS])),
                [b_e4], [b_sgp])
            for hf_ in range(2):
                pm, bpm = next_mm()
                pe(lambda e, pm=pm, hf_=hf_: e.matmul(pm[0:64, :], Pm, x4[:, hf_ * 8:(hf_ + 1) * 8].rearrange(
                    "p g t s -> p (g t s)"), start=True, stop=True), [b_sgp, b_cst], [bpm])
                dve(lambda e, pm=pm, hf_=hf_: e.tensor_tensor(
                    out=wts[:, hf_ * 8:(hf_ + 1) * 8, :], in0=pm[0:64, :].rearrange("p (g n) -> p g n", g=8),
                    in1=Dm.unsqueeze(1).to_broadcast([64, 8, 64]), op=ALU.mult), [bpm, b_cst], [b_wt])
        dma(lngb, sg_ln_g[j].partition_broadcast(128), b_lnbc, writes=[b_lnbc, bf("scr_hi")])
        dma(lnbb, sg_ln_b[j].partition_broadcast(128), b_lnbc, writes=[b_lnbc, bf("scr_hi")])

    b_y = bf("ybuf"); b_wres = bf("wres"); b_hin = bf("hin"); b_lt = bf("ltmp"); b_carry = bf("carry")
    b_hsm = bf("hsm"); b_csm = bf("csm")

    prep_done = {}

    def lru_mixer(l, j, tiles, first_group, has_sample):
        if not prep_done.get(("lru", l)):
            prep_lru(j)
        if has_sample:
            dma(st_in[0:NS, :], h0s[j], b_stin, writes=[b_stin, b_fst])
            for h in range(NH):
                transpose_to(hsm[:, j, h, :], bf("hsm_%d_%d" % (j, h)), st_in[0:NS, h * HB:(h + 1) * HB], b_stin, NS, HB, eng="dve")
            dma(st_in[0:3 * NS, :], cvs[j], b_stin, writes=[b_stin, b_fst])
            for h in range(NH):
                transpose_to(csm[:, j, h, :], bf("csm_%d_%d" % (j, h)), st_in[0:3 * NS, h * HB:(h + 1) * HB], b_stin, 3 * NS, HB, eng="dve")
        win = lru_w_in[j].rearrange("(k p) n -> p k n", p=128)
        lt_all = [bf("lt%d_%s" % (ss, nm)) for ss in range(3) for nm in ("xpb", "xc", "r", "ig", "a", "hs", "gl", "xcb")]
        inherit(lt_all, [b_vbf, b_vg, b_lnbc, bf("scr_hi"), b_ngt, b_wt, b_sgp])
        items = [(h, tl) for h in range(NH) for tl in tiles]
        ctx = {}

        def load_head(hh_):
            ctx[hh_] = [load_unit(win[:, :, (DR if br_ == 0 else 0) + hh_ * HB:(DR if br_ == 0 else 0) + (hh_ + 1) * HB], 128,
                                  "p (k n) -> p k n", k=8) for br_ in range(2)]
        cast_cfg["engs"] = [ACT, POOL, ACT]
        load_head(0)
        load_head(1)

        def stageA(idx):
            h, tl = items[idx]
            n, c0 = tl.n, tl.c0
            ns = 1 if tl.kind == "p" else NS
            hl = 3 * ns
            sset = idx % 3
            REG, base = (SCR, sset * 4224) if sset < 2 else (SG2, 0)
            T = dict(h=h, tl=tl, n=n, ns=ns, hl=hl)
            T["xpb"] = xpb = REG[0:HB, base:base + 520]
            T["xc"], T["r"], T["ig"], T["a"], T["hs"], T["gl"] = [REG[0:HB, base + 520 + q * 512:base + 520 + (q + 1) * 512]
                                                                  for q in range(6)]
            T["xcb"] = xcb = REG[0:HB, base + 3592:base + 3848].bitcast(BF16)
            for nm in ("xpb", "xc", "r", "ig", "a", "hs", "gl", "xcb"):
                T["b" + nm] = bf("lt%d_%s" % (sset, nm))
            xc, gl = T["xc"], T["gl"]
            T["b_ch"] = b_ch = bf("carry_%d_%d" % (j, h)); T["b_hs_"] = bf("hsm_%d_%d" % (j, h))
            T["b_cs_"] = b_cs_ = bf("csm_%d_%d" % (j, h))
            if tl is tiles[0] and h + 2 < NH:
                load_head(h + 2)
            (wx_, bwx_), (wg_, bwg_) = ctx[h]
            pmx, bpx = next_mm()
            pe(lambda e: [e.matmul(pmx[0:HB, 0:n], wx_[:, k, :], hin[:, k, c0:c0 + n], start=(k == 0), stop=(k == 7))
                          for k in range(8)][-1], [bwx_, b_hin], [bpx])
            pmg, bpg = next_mm()
            pe(lambda e: [e.matmul(pmg[0:HB, 0:n], wg_[:, k, :], hin[:, k, c0:c0 + n], start=(k == 0), stop=(k == 7))
                          for k in range(8)][-1], [bwg_, b_hin], [bpg])
            if tl.kind == "p":
                dve(lambda e: e.tensor_copy(out=xpb[:, 0:3], in_=carry_c[:, j, :, h]), [b_ch], [T["bxpb"]])
            else:
                dve(lambda e: e.tensor_copy(out=xpb[:, 0:hl], in_=csm[:, j, h, :]), [b_cs_], [T["bxpb"]])
            act(lambda e: e.activation(out=xpb[:, hl:hl + n], in_=pmx[0:HB, 0:n], func=AF.Copy), [bpx], [T["bxpb"]])
            act(lambda e: e.activation(out=gl[:, 0:n], in_=pmg[0:HB, 0:n], func=AF.Gelu_apprx_tanh), [bpg], [T["bgl"]])
            dve(lambda e: e.tensor_scalar(out=xc[:, 0:n], in0=xpb[:, 0:n], scalar1=lv(0, h), scalar2=lv(4, h),
                                          op0=ALU.mult, op1=ALU.add), [T["bxpb"], b_lvec], [T["bxc"]])
            for kk in range(1, 4):
                dve(lambda e, kk=kk: e.scalar_tensor_tensor(out=xc[:, 0:n], in0=xpb[:, kk * ns:kk * ns + n], scalar=lv(kk, h),
                                                            in1=xc[:, 0:n], op0=ALU.mult, op1=ALU.add),
                    [T["bxpb"], b_lvec], [T["bxc"]])
            if tl.kind == "p":
                dve(lambda e: e.tensor_copy(out=carry_c[:, j, :, h], in_=xpb[:, n:n + 3]), [T["bxpb"]], [b_ch])
            else:
                dve(lambda e: e.tensor_copy(out=csm[:, j, h, :], in_=xpb[:, n:n + hl]), [T["bxpb"]], [b_cs_])
            pool(lambda e: e.tensor_copy(out=xcb[:, 0:n], in_=xc[:, 0:n]), [T["bxc"]], [T["bxcb"]])
            return T

        def stageB(T):
            h, tl, n = T["h"], T["tl"], T["n"]
            c0 = tl.c0
            xc, r_, ig, a_, hs_, gl = T["xc"], T["r"], T["ig"], T["a"], T["hs"], T["gl"]
            bxc, br, big_, ba_, bhs, bgl = T["bxc"], T["br"], T["big"], T["ba"], T["bhs"], T["bgl"]
            b_ch, b_hs_ = T["b_ch"], T["b_hs_"]
            xcb = T["xcb"]
            pma, bpa = next_mm()
            pe(lambda e: e.matmul(pma[0:HB, 0:n], gw[:, 0, h, :], xcb[:, 0:n], start=True, stop=True), [b_gw, T["bxcb"]], [bpa])
            pmi, bpi = next_mm()
            pe(lambda e: e.matmul(pmi[0:HB, 0:n], gw[:, 1, h, :], xcb[:, 0:n], start=True, stop=True), [b_gw, T["bxcb"]], [bpi])
            act(lambda e: e.activation(out=r_[:, 0:n], in_=pma[0:HB, 0:n], func=AF.Tanh, bias=lv(5, h), scale=0.5),
                [bpa, b_lvec], [br])
            act(lambda e: e.activation(out=ig[:, 0:n], in_=pmi[0:HB, 0:n], func=AF.Tanh, bias=lv(6, h), scale=0.5),
                [bpi, b_lvec], [big_])
            act(lambda e: e.activation(out=a_[:, 0:n], in_=r_[:, 0:n], func=AF.Exp, scale=lv(7, h), bias=lv(7, h)),
                [br, b_lvec], [ba_])
            act(lambda e: e.activation(out=r_[:, 0:n], in_=a_[:, 0:n], func=AF.Square), [ba_], [br])
            act(lambda e: e.activation(out=r_[:, 0:n], in_=r_[:, 0:n], func=AF.Ln, scale=-1.0, bias=1.0), [br], [br])
            act(lambda e: e.activation(out=r_[:, 0:n], in_=r_[:, 0:n], func=AF.Exp, scale=0.5, bias=LN_HALF), [br], [br])
            dve(lambda e: e.tensor_tensor(out=r_[:, 0:n], in0=r_[:, 0:n], in1=xc[:, 0:n], op=ALU.mult), [br, bxc], [br])
            dve(lambda e: e.scalar_tensor_tensor(out=ig[:, 0:n], in0=ig[:, 0:n], scalar=1.0, in1=r_[:, 0:n], op0=ALU.add,
                                                 op1=ALU.mult), [big_, br], [big_])
            if tl.kind == "p":
                dve(lambda e: e.tensor_tensor_scan(out=hs_[:, 0:n], data0=a_[:, 0:n], data1=ig[:, 0:n],
                                                   initial=carry_h[:, j, h, 0:1], op0=ALU.mult, op1=ALU.add),
                    [ba_, big_, b_ch], [bhs])
                dve(lambda e: e.tensor_copy(out=carry_h[:, j, h, 0:1], in_=hs_[:, n - 1:n]), [bhs], [b_ch])
            else:
                for t in range(TS):
                    prev = hsm[:, j, h, :] if t == 0 else hs_[:, (t - 1) * NS:t * NS]
                    dve(lambda e, t=t, prev=prev: e.tensor_tensor(out=hs_[:, t * NS:(t + 1) * NS], in0=a_[:, t * NS:(t + 1) * NS],
                                                                in1=prev, op=ALU.mult), [ba_, b_hs_, bhs], [bhs])
                    dve(lambda e, t=t: e.tensor_tensor(out=hs_[:, t * NS:(t + 1) * NS], in0=hs_[:, t * NS:(t + 1) * NS],
                                                     in1=ig[:, t * NS:(t + 1) * NS], op=ALU.add), [bhs, big_], [bhs])
                dve(lambda e: e.tensor_copy(out=hsm[:, j, h, :], in_=hs_[:, (TS - 1) * NS:TS * NS]), [bhs], [b_hs_])
            dve(lambda e: e.tensor_tensor(out=ybuf[0:HB, h, c0:c0 + n], in0=hs_[:, 0:n], in1=gl[:, 0:n], op=ALU.mult),
                [bhs, bgl], [b_y, b_hid])

        DEPTH_ = 2
        Ts = {i_: stageA(i_) for i_ in range(min(DEPTH_, len(items)))}
        for idx in range(len(items)):
            ada_spread(2)
            stageB(Ts.pop(idx))
            if idx + DEPTH_ < len(items):
                Ts[idx + DEPTH_] = stageA(idx + DEPTH_)
        cast_cfg["engs"] = [ACT, DVE, ACT, DVE, POOL, ACT, DVE]

    b_vbf = bf("vbf"); b_vg = bf("vg"); b_ln = bf("ln")

    def sgu_mixer(l, j, tiles, has_sample):
        lt_all = [bf("lt%d_%s" % (ss, nm)) for ss in range(3) for nm in ("xpb", "xc", "r", "ig", "a", "hs", "gl", "xcb")]
        if not prep_done.get(("sgu", l)):
            inherit([b_vbf, b_vg, b_lnbc, b_wt, b_sgp], lt_all)
            prep_sgu(j, has_sample)
        win = sg_w_in[j].rearrange("(k p) n -> p k n", p=128)
        for q in range(12):
            st, sbf = next_stg()
            stv = st[:, :].rearrange("p (k n) -> p k n", k=8)
            dma(stv, win[:, :, DR + q * 128:DR + (q + 1) * 128], sbf, writes=[sbf])
            cast(wv[:, :, q * 128:(q + 1) * 128], stv, [sbf], [b_wres, b_hid, b_y])
        chunks = []
        for tl in tiles:
            if tl.kind == "p":
                for cc in range(tl.n // 128):
                    chunks.append((tl, tl.c0 + cc * 128, 128, cc))
            else:
                chunks.append((tl, tl.c0, NSMP, 0))
        for ci, (tl, cc0, cn, _) in enumerate(chunks):
            for ct in range(3):
                pm, bpm = next_mm()

                def mmv(e, pm=pm, ct=ct):
                    for k in range(8):
                        e.matmul(pm[0:cn, :], hin[:, k, cc0:cc0 + cn], wv[:, k, ct * 512:(ct + 1) * 512], start=(k == 0),
                                 stop=False)
                    return e.matmul(pm[0:cn, :], ones_b[0:1, 0:cn], bvrow[0:1, ct * 512:(ct + 1) * 512], start=False,
                                    stop=True)
                pe(mmv, [b_hin, b_wres, b_sgp, b_c], [bpm])
                act(lambda e, pm=pm, ct=ct: e.activation(out=vg[0:cn, ct * 512:(ct + 1) * 512], in_=pm[0:cn, :],
                                                        func=AF.Gelu_apprx_tanh), [bpm], [b_vg])
                dve(lambda e, ct=ct: e.bn_stats(out=lnst[0:cn, ct, :], in_=vg[0:cn, ct * 512:(ct + 1) * 512]), [b_vg], [b_ln])
            dve(lambda e: e.bn_aggr(out=lnag[0:cn, :], in_=lnst[0:cn, :, :].rearrange("p a b -> p (a b)")), [b_ln], [b_ln])
            act(lambda e: e.activation(out=lnr[0:cn, 0:1], in_=lnag[0:cn, 1:2], func=AF.Ln, bias=EPS, scale=1.0), [b_ln], [b_ln])
            act(lambda e: e.activation(out=lnr[0:cn, 1:2], in_=lnr[0:cn, 0:1], func=AF.Exp, scale=-0.5), [b_ln], [b_ln])
            dve(lambda e: e.scalar_tensor_tensor(out=vg[0:cn, :], in0=vg[0:cn, :], scalar=lnag[0:cn, 0:1], in1=lngb[0:cn, :],
                                                 op0=ALU.subtract, op1=ALU.mult), [b_vg, b_ln, b_lnbc], [b_vg])
            if tl.kind == "p":
                dve(lambda e, ci=ci: e.scalar_tensor_tensor(out=vbf[0:cn, ci, :], in0=vg[0:cn, :], scalar=lnr[0:cn, 1:2],
                                                          in1=lnbb[0:cn, :], op0=ALU.mult, op1=ALU.add),
                    [b_vg, b_ln, b_lnbc], [b_vbf])
            else:
                dve(lambda e: e.scalar_tensor_tensor(out=vg[0:cn, :], in0=vg[0:cn, :], scalar=lnr[0:cn, 1:2],
                                                     in1=lnbb[0:cn, :], op0=ALU.mult, op1=ALU.add),
                    [b_vg, b_ln, b_lnbc], [b_vg])
                act(lambda e, ci=ci: e.activation(out=vbf[0:cn, ci, :], in_=vg[0:cn, :], func=AF.Copy), [b_vg], [b_vbf])
                dma(vs_o[j], vg[0:cn, :], b_out, reads=[b_vg])
        def u_consume(g, hd_):
            wuv, bwu = hd_
            for tl in tiles:
                n, c0 = tl.n, tl.c0
                pmu, bpu = next_mm()
                pe(lambda e, pmu=pmu, wuv=wuv: [e.matmul(pmu[0:HB, 0:n], wuv[:, k, :], hin[:, k, c0:c0 + n], start=(k == 0),
                                                        stop=(k == 7)) for k in range(8)][-1], [bwu, b_hin], [bpu])
                i = cnt["ta"] % 2
                cnt["ta"] += 1
                u_, bu_ = tmpa[i][0:HB, :], bf("tmpa%d" % i)
                act(lambda e, pmu=pmu, u_=u_: e.activation(out=u_[:, 0:n], in_=pmu[0:HB, 0:n], func=AF.Gelu_apprx_tanh,
                                                          bias=sgbu[:, g:g + 1], scale=1.0), [bpu, b_sgp], [bu_])
                pmm, bpm = next_mm()

                def mmix(e, pmm=pmm):
                    ins = None
                    for ci, (ctl, cc0, cn, cc) in enumerate(chunks):
                        if ctl is not tl:
                            continue
                        off = cc0 - c0
                        if ctl.kind == "p":
                            e.matmul(pmm[0:HB, off:off + cn], vbf[0:cn, ci, g * HB:(g + 1) * HB], wtg[:, g, :], start=True,
                                     stop=False)
                            ins = e.matmul(pmm[0:HB, off:off + cn], ones_b[0:1, 0:HB], bsr[0:1, g, :], start=False, stop=True)
                        else:
                            e.matmul(pmm[0:HB, off:off + cn], vbf[0:cn, ci, g * HB:(g + 1) * HB], wts[:, g, :], start=True,
                                     stop=False)
                            ins = e.matmul(pmm[0:HB, off:off + cn], ones_b[0:1, 0:HB],
                                           bss[0:1, g].rearrange("p t s -> p (t s)"), start=False, stop=True)
                    return ins
                pe(mmix, [b_vbf, b_wt, b_sgp, b_c], [bpm])
                dve(lambda e, pmm=pmm, u_=u_: e.tensor_tensor(out=ybuf[0:HB, g, c0:c0 + n], in0=u_[:, 0:n],
                                                            in1=pmm[0:HB, 0:n], op=ALU.mult), [bpm, bu_], [b_y, b_hid])

        stream([(lambda g=g: load_unit(win[:, :, g * HB:(g + 1) * HB], 128, "p (k n) -> p k n", k=8)) for g in range(NH)],
               u_consume, PF=5)


    def mlp(l, tiles):
        wup = mlp_w_up[l].rearrange("(k p) n -> p k n", p=128)
        wdn = mlp_w_down[l].rearrange("(h p) n -> p h n", p=128)
        def up_consume(hc, hd_):
            wuv, bwu = hd_
            ada_spread(1)
            for tl in tiles:
                n, c0 = tl.n, tl.c0
                pm, bpm = next_mm()
                pe(lambda e, pm=pm, n=n, c0=c0: [e.matmul(pm[:, 0:n], wuv[:, k, :], hin[:, k, c0:c0 + n],
                                                         start=(k == 0), stop=(k == 7)) for k in range(8)][-1],
                   [bwu, b_hin], [bpm])
                i = cnt["ta"] % 2
                cnt["ta"] += 1
                ta, bta = tmpa[i], bf("tmpa%d" % i)
                act(lambda e, pm=pm, ta=ta, n=n: e.activation(out=ta[:, 0:n], in_=pm[:, 0:n], func=AF.Relu), [bpm], [bta])
                E = dve
                E(lambda e, ta=ta, n=n, c0=c0: e.tensor_tensor(out=hidden[:, hc, c0:c0 + n], in0=ta[:, 0:n],
                                                             in1=ta[:, 0:n], op=ALU.mult), [bta], [b_hid, b_wres, b_y])
        stream([(lambda hc=hc: load_unit(wup[:, :, hc * 128:(hc + 1) * 128], 128, "p (k n) -> p k n", k=8))
                for hc in range(32)], up_consume)
        dstate = {}

        def dn_consume(i, hd_):
            oc, q = i // 4, i % 4
            wdv, bwd = hd_
            if q == 0:
                dstate["pms"] = [next_mm() for _ in tiles]
            pms = dstate["pms"]
            for ti, tl in enumerate(tiles):
                n, c0 = tl.n, tl.c0
                pm, bpm = pms[ti]
                pe(lambda e, pm=pm, n=n, c0=c0: [e.matmul(pm[:, 0:n], wdv[:, hh, :], hidden[:, q * 8 + hh, c0:c0 + n],
                                                         start=(q == 0 and hh == 0), stop=(q == 3 and hh == 7))
                                                for hh in range(8)][-1], [bwd, b_hid], [bpm])
            if q == 3:
                for ti, tl in enumerate(tiles):
                    n, c0 = tl.n, tl.c0
                    pm, bpm = pms[ti]
                    act(lambda e, pm=pm, n=n, c0=c0: e.activation(out=fstore[:, oc, c0:c0 + n], in_=pm[:, 0:n], func=AF.Copy),
                        [bpm], [b_fst, b_hin, b_stin])
        stream([(lambda oc=oc, q=q: load_unit(wdn[:, q * 8:(q + 1) * 8, oc * 128:(oc + 1) * 128], 128, "p (h n) -> p h n", h=8))
                for oc in range(8) for q in range(4)], dn_consume)
        for tl in tiles:
            post_norm_residual(l, tl, lambda k, tl=tl: fstore[:, k, tl.c0:tl.c0 + tl.n], [b_fst], 5)

    last_sample_group = max(gi for gi, g in enumerate(GROUPS) if any(t[0] == "s" for t in g))
    last_prompt_group = max(gi for gi, g in enumerate(GROUPS) if any(t[0] == "p" for t in g))
    for gi, gdef in enumerate(GROUPS):
        tiles = []
        c0 = 0
        for (kind, off) in gdef:
            tl = Tile()
            tl.kind, tl.off, tl.c0 = kind, off, c0
            tl.n = 512 if kind == "p" else NSMP
            tl.bx = bf("xg_%d_%d" % (gi, c0))
            tiles.append(tl)
            c0 += tl.n
        has_sample = any(t.kind == "s" for t in tiles)
        inherit([b_xtok], ada_xt)
        for tl in tiles:
            nblk = tl.n // 128 if tl.kind == "p" else 1
            for bi in range(nblk):
                rows = 128 if tl.kind == "p" else NSMP
                src = xp[tl.off + bi * 128:tl.off + (bi + 1) * 128, :] if tl.kind == "p" else xs[:, :]
                if bi % 2 == 0:
                    halves = [(xtok[0:rows, 0:512], b_xtok), (xtok[0:rows, 512:1024], b_xtok)]
                else:
                    halves = [(tmpa[0][0:rows, :], bf("tmpa0")), (tmpa[1][0:rows, :], bf("tmpa1"))]
                for hv, (hap, hb) in enumerate(halves):
                    dbs = (b_xtok, b_xtok2) if hb is b_xtok else (b_xalt, b_xalt2)
                    dma(hap, src[:, hv * 512:(hv + 1) * 512], dbs[hv], writes=[hb])
                for kq in range(2):
                    pt, bpt = next_tr()
                    hap, hb = halves[kq]

                    def trx(e, pt=pt, kq=kq, rows=rows, hap=hap):
                        for kk in range(4):
                            ins = e.transpose(pt[:, kk * rows:(kk + 1) * rows], hap[:, kk * 128:(kk + 1) * 128],
                                              ident[0:rows, 0:rows])
                        return ins
                    pe(trx, [hb, b_cst], [bpt])
                    cdst = tl.c0 + bi * 128
                    act(lambda e, pt=pt, kq=kq, rows=rows, cdst=cdst: e.activation(
                        out=xg[:, kq * 4:(kq + 1) * 4, cdst:cdst + rows],
                        in_=pt[:, 0:4 * rows].rearrange("p (k n) -> p k n", k=4), func=AF.Copy), [bpt], [tl.bx])
        inherit(ada_xt, [b_xtok])
        for l in range(4):
            j = l // 2
            while ada_pending["l"] <= l:
                ada_spread(1)
            for tl in tiles:
                pre_norm(l, tl, 1, 0, [b_hin, b_fst])
            if l % 2 == 0:
                lru_mixer(l, j, tiles, gi == 0, has_sample)
                lt_all_ = [bf("lt%d_%s" % (ss, nm)) for ss in range(3) for nm in ("xpb", "xc", "r", "ig", "a", "hs", "gl", "xcb")]
                inherit([b_vbf, b_vg, b_lnbc, b_wt, b_sgp], lt_all_)
                prep_sgu(j, has_sample)
                prep_done[("sgu", l + 1)] = True
            else:
                sgu_mixer(l, j, tiles, has_sample)
                if l + 1 < 4:
                    prep_lru(j + 1)
                    prep_done[("lru", l + 1)] = True
            w_out_j = (lru_w_out if l % 2 == 0 else sg_w_out)[j]
            pstate = {}

            def p3_consume(i, hd_, pstate=pstate):
                oc, q = i // 2, i % 2
                wov, bwo = hd_
                if q == 0:
                    pstate["pms"] = [next_mm() for _ in tiles]
                pms = pstate["pms"]
                for ti, tl in enumerate(tiles):
                    n, c0_ = tl.n, tl.c0
                    pm, bpm = pms[ti]
                    pe(lambda e, pm=pm, c0_=c0_, n=n: [e.matmul(pm[:, 0:n], wov[:, hh, :], ybuf[0:HB, q * 8 + hh, c0_:c0_ + n],
                                                               start=(q == 0 and hh == 0), stop=(q == 1 and hh == 7))
                                                      for hh in range(8)][-1], [bwo, b_y], [bpm])
                if q == 1:
                    for ti, tl in enumerate(tiles):
                        n, c0_ = tl.n, tl.c0
                        pm, bpm = pms[ti]
                        act(lambda e, pm=pm, c0_=c0_, n=n: e.activation(out=fstore[:, oc, c0_:c0_ + n], in_=pm[:, 0:n],
                                                                      func=AF.Copy), [bpm], [b_fst, b_hin, b_stin])
            stream([(lambda oc=oc, q=q: load_unit(
                w_out_j[q * 8 * HB:(q + 1) * 8 * HB, oc * 128:(oc + 1) * 128].rearrange("(h c) n -> c h n", c=HB), HB,
                "p (h n) -> p h n", h=8)) for oc in range(8) for q in range(2)], p3_consume)
            for tl in tiles:
                post_norm_residual(l, tl, lambda k, tl=tl: fstore[:, k, tl.c0:tl.c0 + tl.n], [b_fst], 2)
            for tl in tiles:
                pre_norm(l, tl, 4, 3, [b_hin, b_fst])
            mlp(l, tiles)
        inherit([b_xtok], ada_xt)
        for tl in tiles:
            nblk = tl.n // 128 if tl.kind == "p" else 1
            for bi in range(nblk):
                rows = 128 if tl.kind == "p" else NSMP
                csrc = tl.c0 + bi * 128
                for kq in range(2):
                    pt, bpt = next_tr()

                    def try_(e, pt=pt, kq=kq, rows=rows, csrc=csrc):
                        for kk in range(4):
                            k = kq * 4 + kk
                            ins = e.transpose(pt[0:rows, kk * 128:(kk + 1) * 128], xg[:, k, csrc:csrc + rows], ident[:, :])
                        return ins
                    pe(try_, [tl.bx, b_cst], [bpt])
                    act(lambda e, pt=pt, kq=kq, rows=rows: e.activation(out=xtok[0:rows, kq * 512:(kq + 1) * 512],
                                                                      in_=pt[0:rows, :], func=AF.Copy), [bpt], [b_xtok])
                dst = yp[tl.off + bi * 128:tl.off + (bi + 1) * 128, :] if tl.kind == "p" else ys[:, :]
                dma(dst, xtok[0:rows, :], b_xtok, reads=[b_xtok])
        if gi == last_prompt_group:
            for j in range(2):
                transpose_to(st_in[0:NH, 0:HB], b_stin, carry_h[:, j, :, 0], [bf("carry_%d_%d" % (j, h_)) for h_ in range(NH)], HB, NH)
                dma(hp_o[j].rearrange("(h c) -> h c", c=HB), st_in[0:NH, 0:HB], b_stin, reads=[b_stin])
                transpose_to(st_in[0:3 * NH, 0:HB], b_stin, carry_c[:, j].rearrange("p k h -> p (k h)"),
                             [bf("carry_%d_%d" % (j, h_)) for h_ in range(NH)], HB, 3 * NH)
                for kk in range(3):
                    dma(cp_o[j, kk].rearrange("(h c) -> h c", c=HB), st_in[kk * NH:(kk + 1) * NH, 0:HB], b_stin,
                        reads=[b_stin])
        if gi == last_sample_group:
            for j in range(2):
                for h in range(NH):
                    transpose_to(st_in[0:NS, h * HB:(h + 1) * HB], b_stin, hsm[:, j, h, :], bf("hsm_%d_%d" % (j, h)), HB, NS)
                dma(hs_o[j], st_in[0:NS, :], b_stin, reads=[b_stin])
                for h in range(NH):
                    transpose_to(st_in[0:3 * NS, h * HB:(h + 1) * HB], b_stin, csm[:, j, h, :], bf("csm_%d_%d" % (j, h)), HB, 3 * NS)
                dma(cs_o[j], st_in[0:3 * NS, :], b_stin, reads=[b_stin])

    for b in dsems:
        if b.dcnt:
            SP.prog.append(("wait", b.dsem, 16 * b.dcnt))

    def replay(E):
        def body(eng):
            for op in E.prog:
                if op[0] == "wait":
                    eng.wait_ge(op[1], op[2])
                elif op[0] == "dma":
                    eng.dma_start(out=op[1], in_=op[2], **op[3]).then_inc(op[4], 16)
                else:
                    ins = None
                    for (name, a, kw) in op[1]:
                        ins = getattr(eng, name)(*a, **kw)
                    ins.then_inc(E.sem, 1)
        return body

    with nc.Block() as block:
        block.sync(replay(SP))
        block.tensor(replay(PE))
        block.scalar(replay(ACT))
        block.vector(replay(DVE))
        block.gpsimd(replay(POOL))
    es.close()
    return nc


_CACHE = {}


def _consts():
    c = np.zeros((128, 384), np.float32)
    c[:, 0:128] = np.eye(128, dtype=np.float32)
    c[:, 128:256] = np.tril(np.ones((128, 128), np.float32))
    p = np.arange(64)
    t2, s2 = p // NS, p % NS
    c[0:64, 256:320] = ((s2[:, None] == s2[None, :]) & (t2[:, None] <= t2[None, :])).astype(np.float32)
    c[0:4, 320:384] = (np.arange(4)[:, None] == t2[None, :]).astype(np.float32)
    return c


def kernel(**inp):
    if "nc" not in _CACHE:
        _CACHE["nc"] = build_nc()
    nc = _CACHE["nc"]
    f = lambda a: np.ascontiguousarray(np.asarray(a, dtype=np.float32))
    shared = {
        "ada_w": f(inp["ada_w"]), "ada_b": f(inp["ada_b"]), "norm_g": f(inp["norm_g"]).reshape(16, D),
        "lru_w_in": f(inp["lru_w_in"]), "lru_conv_w": f(inp["lru_conv_w"]), "lru_conv_b": f(inp["lru_conv_b"]),
        "lru_wa": f(inp["lru_wa"]), "lru_ba": f(inp["lru_ba"]).reshape(2, DR), "lru_wx": f(inp["lru_wx"]),
        "lru_bx": f(inp["lru_bx"]).reshape(2, DR), "lru_lambda": f(inp["lru_lambda"]), "lru_w_out": f(inp["lru_w_out"]),
        "sg_w_in": f(inp["sg_w_in"]), "sg_b_in": f(inp["sg_b_in"]), "sg_ln_g": f(inp["sg_ln_g"]), "sg_ln_b": f(inp["sg_ln_b"]),
        "sg_ws": f(inp["sg_ws"]), "sg_bs": f(inp["sg_bs"]), "sg_w_out": f(inp["sg_w_out"]),
        "mlp_w_up": f(inp["mlp_w_up"]), "mlp_w_down": f(inp["mlp_w_down"]), "cst": _consts(),
    }
    x_prompt = f(inp["x_prompt"]); x_sample = f(inp["x_sample"]); c_prompt = f(inp["c_prompt"]); c_sample = f(inp["c_sample"])
    sh = f(inp["state_lru_h"]); sc = f(inp["state_lru_conv"])
    in_maps = []
    for c in range(NCORE):
        sl = slice(c * NS, (c + 1) * NS)
        m = dict(shared)
        m["xp"] = x_prompt[c]
        m["xs"] = np.ascontiguousarray(x_sample[sl].transpose(1, 0, 2).reshape(NSMP, D))
        m["c17"] = np.ascontiguousarray(np.concatenate([c_prompt[c:c + 1], c_sample[sl]], 0))
        m["h0s"] = np.ascontiguousarray(sh[:, sl])
        m["cvs"] = np.ascontiguousarray(sc[:, sl].transpose(0, 2, 1, 3).reshape(2, 3 * NS, DR))
        in_maps.append(m)
    res = run_bass_kernel_spmd(nc, in_maps, core_ids=list(range(NCORE)))
    R = res.results
    y_prompt = np.stack([R[c]["yp"] for c in range(NCORE)], 0)
    y_sample = np.concatenate([R[c]["ys"].reshape(TS, NS, D).transpose(1, 0, 2) for c in range(NCORE)], 0)
    h_prompt = np.stack([R[c]["hp"] for c in range(NCORE)], 1)
    conv_prompt = np.stack([R[c]["cp"] for c in range(NCORE)], 1)
    h_sample = np.concatenate([R[c]["hs"] for c in range(NCORE)], 1)
    conv_sample = np.concatenate([R[c]["cs"].reshape(2, 3, NS, DR).transpose(0, 2, 1, 3) for c in range(NCORE)], 1)
    v_sample = np.concatenate([R[c]["vs"].reshape(2, TS, NS, DR).transpose(0, 2, 1, 3) for c in range(NCORE)], 1)
    return tuple(np.ascontiguousarray(a.astype(np.float32)) for a in
                 (y_prompt, y_sample, h_prompt, conv_prompt, h_sample, conv_sample, v_sample))
```

```python
import numpy as np
import concourse.bass as bass
import concourse.mybir as mybir
from concourse.bass_utils import run_bass_kernel_spmd

F32 = mybir.dt.float32
BF16 = mybir.dt.bfloat16
F32R = mybir.dt.float32r
AF = mybir.ActivationFunctionType
ALU = mybir.AluOpType

D = 1024
DR = 1536
NH = 16
HB = 96
DFF = 4096
EPS = 1e-6
LN_HALF = -0.6931471805599453
NCORE = 8
SEQ = 2048
NS = 16
TS = 4
NSMP = NS * TS

GROUPS = [[("p", 0)], [("p", 512)], [("p", 1024)], [("p", 1536), ("s", 0)]]
NG = 576


class Buf:
    __slots__ = ("w", "r", "name", "dsem", "dcnt")

    def __init__(self, name):
        self.w = None
        self.r = {}
        self.name = name
        self.dsem = None
        self.dcnt = 0


class Eng:
    def __init__(self, name, sem, is_pe=False):
        self.name = name
        self.sem = sem
        self.n = 0
        self.seen = {}
        self.is_pe = is_pe
        self.prog = []


class Rec:
    def __init__(self):
        self.calls = []

    def __getattr__(self, name):
        def f(*a, **kw):
            self.calls.append((name, a, kw))
            return len(self.calls) - 1
        return f


def build_nc():
    nc = bass.Bass("TRN2", target_bir_lowering=False)

    def din(name, shape):
        return nc.dram_tensor(name, list(shape), F32, kind="ExternalInput").ap()

    def dout(name, shape):
        return nc.dram_tensor(name, list(shape), F32, kind="ExternalOutput").ap()

    xp = din("xp", [SEQ, D]); xs = din("xs", [NSMP, D]); c17 = din("c17", [17, D])
    h0s = din("h0s", [2, NS, DR]); cvs = din("cvs", [2, 3 * NS, DR])
    ada_w = din("ada_w", [4, D, 6 * D]); ada_b = din("ada_b", [4, 6 * D]); norm_g = din("norm_g", [16, D])
    lru_w_in = din("lru_w_in", [2, D, 2 * DR]); lru_conv_w = din("lru_conv_w", [2, 4, DR])
    lru_conv_b = din("lru_conv_b", [2, DR]); lru_wa = din("lru_wa", [2, NH, HB, HB]); lru_ba = din("lru_ba", [2, DR])
    lru_wx = din("lru_wx", [2, NH, HB, HB]); lru_bx = din("lru_bx", [2, DR]); lru_lam = din("lru_lambda", [2, DR])
    lru_w_out = din("lru_w_out", [2, DR, D]); sg_w_in = din("sg_w_in", [2, D, 2 * DR]); sg_b_in = din("sg_b_in", [2, 2 * DR])
    sg_ln_g = din("sg_ln_g", [2, DR]); sg_ln_b = din("sg_ln_b", [2, DR]); sg_ws = din("sg_ws", [2, NH, 128, 128])
    sg_bs = din("sg_bs", [2, NH, 128]); sg_w_out = din("sg_w_out", [2, DR, D])
    mlp_w_up = din("mlp_w_up", [4, D, DFF]); mlp_w_down = din("mlp_w_down", [4, DFF, D])
    cst = din("cst", [128, 384])

    yp = dout("yp", [SEQ, D]); ys = dout("ys", [NSMP, D]); hp_o = dout("hp", [2, DR]); cp_o = dout("cp", [2, 3, DR])
    hs_o = dout("hs", [2, NS, DR]); cs_o = dout("cs", [2, 3 * NS, DR]); vs_o = dout("vs", [2, NSMP, DR])

    from contextlib import ExitStack
    es = ExitStack()

    def sb(name, shape, dt):
        return es.enter_context(nc.sbuf_tensor(name, list(shape), dt))

    def sem(name):
        return es.enter_context(nc.semaphore(name))

    xg = sb("xg", [128, 8, NG], F32)
    RH = sb("RH", [128, 8 * NG], F32)
    hin = RH[:, 0:4 * NG].bitcast(BF16).rearrange("p (k n) -> p k n", k=8)
    fstore = RH[:, :].rearrange("p (k n) -> p k n", k=8)
    NBIG = max(32 * NG, 16 * NG + 8 * DR)
    BIG = sb("BIG", [128, NBIG], BF16)
    hidden = BIG[:, 0:32 * NG].rearrange("p (h n) -> p h n", h=32)
    ybuf = BIG[:, 0:16 * NG].rearrange("p (h n) -> p h n", h=16)
    wres = BIG[:, 16 * NG:16 * NG + 8 * DR]
    wv = wres[:, 0:8 * DR].rearrange("p (k n) -> p k n", k=8)
    SCR = sb("SCR", [128, 8448], F32)
    def ltmp(i):
        return SCR[0:HB, i * 520:(i + 1) * 520]
    vbf = SCR[:, 0:3840].bitcast(BF16).rearrange("p (c n) -> p c n", c=5)
    vg = SCR[:, 3840:5376]
    lngb = SCR[:, 5376:6912]
    lnbb = SCR[:, 6912:8448]
    NSTG = 4
    stg = [sb("stg%d" % i, [128, 1024], F32) for i in range(NSTG)]
    NWS = 6
    wsl = [sb("wsl%d" % i, [128, 1024], BF16) for i in range(NWS)]
    ada_fm = sb("ada_fm", [128, 4, 48, 17], F32)
    cst_t = sb("cst_t", [128, 384], F32)
    ident = cst_t[:, 0:128]; trilm = cst_t[:, 128:256]; Dm = cst_t[0:64, 256:320]; Pm = cst_t[0:4, 320:384]
    ones_r = sb("ones_r", [128, 128], F32R)
    ones_f = sb("ones_f", [128, 128], F32)
    ones_b = sb("ones_b", [128, 128], BF16)
    scT = sb("scT", [128, 8, 17], F32)
    ngT = sb("ngT", [128, 8, 16], F32)
    sqt = [sb("sqt%d" % i, [128, 512], F32R) for i in range(3)]
    rstd = sb("rstd", [128, 512], F32)
    tmpa = [sb("tmpa%d" % i, [128, 512], F32) for i in range(2)]
    lvec_in = sb("lvec_in", [128, HB], F32)
    lvec = sb("lvec", [HB, 128], F32)
    gw = sb("gw", [HB, 2, NH, HB], BF16)
    carry_h = sb("carry_h", [HB, 2, NH, 1], F32)
    carry_c = sb("carry_c", [HB, 2, 3, NH], F32)
    hsm = sb("hsm", [HB, 2, NH, NS], F32)
    csm = sb("csm", [HB, 2, NH, 3 * NS], F32)
    sgv_in = sb("sgv_in", [16, HB], F32)
    sgbu = sb("sgbu", [HB, NH], F32)
    e4 = sb("e4", [4, NH, 4], F32)
    x4 = sb("x4", [4, NH, 4, NS], F32)
    SG2 = sb("SG2", [128, 3848], F32)
    wtg = SG2[:, 0:1024].bitcast(BF16).rearrange("p (g t) -> p g t", g=NH)
    wts = SG2[0:64, 1024:1536].bitcast(BF16).rearrange("p (g t) -> p g t", g=NH)
    bsr = SG2[0:1, 1536:2560].bitcast(BF16).rearrange("p (g t) -> p g t", g=NH)
    bss = SG2[0:1, 2560:3072].bitcast(BF16).rearrange("p (g t s) -> p g t s", g=NH, t=4)
    bvrow = SG2[0:1, 3072:3840].bitcast(BF16)
    lnst = sb("lnst", [128, 3, 6], F32)
    lnag = sb("lnag", [128, 2], F32)
    lnr = sb("lnr", [128, 2], F32)
    xtok = sb("xtok", [128, D], F32)
    st_in = RH[0:64, 4 * NG:4 * NG + DR]
    c17_t = xtok[0:17, 0:D]
    ng_t = SCR[0:16, 0:D]
    modT = xtok[0:17, 0:256]

    ps = [es.enter_context(nc.psum_tensor("ps%d" % i, [128, 512], F32)) for i in range(8)]

    PE = Eng("tensor", sem("s_pe"), True)
    ACT = Eng("scalar", sem("s_act"))
    DVE = Eng("vector", sem("s_dve"))
    POOL = Eng("gpsimd", sem("s_pool"))
    SP = Eng("sync", sem("s_sp"))
    dsems = []

    def new_dbuf(name):
        b = Buf(name)
        b.dsem = sem("d_" + name)
        dsems.append(b)
        return b

    def _waits(E, reads, writes):
        need = {}

        def add(ev):
            if ev is None:
                return
            s, v = ev
            k = id(s)
            if k not in need or need[k][1] < v:
                need[k] = (s, v)
        for b in reads:
            add(b.w)
        for b in writes:
            add(b.w)
            for ev in b.r.values():
                add(ev)
        for k, (s, v) in need.items():
            if E.seen.get(k, 0) >= v:
                continue
            if s is E.sem and E.is_pe:
                continue
            E.prog.append(("wait", s, v))
            E.seen[k] = v

    def _record(ev, reads, writes):
        k = id(ev[0])
        for b in reads:
            if k not in b.r or b.r[k][1] < ev[1]:
                b.r[k] = ev
        for b in writes:
            b.w = ev
            b.r = {}

    def inherit(dst_bufs, src_bufs):
        for d_ in dst_bufs:
            for s_ in src_bufs:
                for ev in ([s_.w] if s_.w else []) + list(s_.r.values()):
                    k = id(ev[0])
                    if k not in d_.r or d_.r[k][1] < ev[1]:
                        d_.r[k] = ev

    def emit(E, fn, reads=(), writes=()):
        _waits(E, reads, writes)
        rec = Rec()
        fn(rec)
        E.n += 1
        E.prog.append(("ins", rec.calls))
        ev = (E.sem, E.n)
        _record(ev, reads, writes)
        return ev

    def dma(out, in_, db, reads=(), writes=(), E=None, **kw):
        E = E or SP
        _waits(E, reads, writes)
        E.prog.append(("dma", out, in_, kw, db.dsem))
        db.dcnt += 1
        ev = (db.dsem, 16 * db.dcnt)
        _record(ev, reads, writes)
        return ev

    B = {}

    def bf(name):
        if name not in B:
            B[name] = Buf(name)
        return B[name]

    b_ps = [bf("ps%d" % i) for i in range(8)]
    b_stg = [new_dbuf("stg%d" % i) for i in range(NSTG)]
    b_wsl = [bf("wsl%d" % i) for i in range(NWS)]
    cnt = {"stg": 0, "wsl": 0, "mm": 0, "cast": 0, "sq": 0, "ta": 0, "tr": 0, "ua": 0}
    b_cst = new_dbuf("cst"); b_xtok = new_dbuf("xtok"); b_small = b_xtok; b_stin = new_dbuf("stin"); b_ngt = new_dbuf("ngt"); b_xalt = new_dbuf("xalt"); b_xalt2 = new_dbuf("xalt2"); b_xtok2 = new_dbuf("xtok2")
    b_hid = bf("hidden"); b_fst = bf("fstore")
    b_out = new_dbuf("outs")

    def next_mm():
        i = (0, 1, 2, 3, 4, 6)[cnt["mm"] % 6]
        cnt["mm"] += 1
        return ps[i], b_ps[i]

    def next_tr():
        i = (7, 5)[cnt["tr"] % 2]
        cnt["tr"] += 1
        return ps[i], b_ps[i]

    cast_cfg = {"engs": [ACT, DVE]}

    def cast(out, in_, reads, writes):
        ce = cast_cfg["engs"]
        E = ce[cnt["cast"] % len(ce)]
        cnt["cast"] += 1
        if E is ACT:
            return emit(E, lambda e: e.activation(out=out, in_=in_, func=AF.Copy), reads, writes)
        return emit(E, lambda e: e.tensor_copy(out=out, in_=in_), reads, writes)

    def load_cast(dst, dst_buf, srcs, np_=128):
        i = cnt["stg"] % NSTG
        cnt["stg"] += 1
        st, sbuf_ = stg[i], b_stg[i]
        views = []
        for (vf, src) in srcs:
            v = vf(st)
            dma(v, src, sbuf_, writes=[sbuf_])
            views.append(v)
        return st, sbuf_

    def act(fn, reads, writes):
        return emit(ACT, fn, reads, writes)

    def dve(fn, reads, writes):
        return emit(DVE, fn, reads, writes)

    def pool(fn, reads, writes):
        return emit(POOL, fn, reads, writes)

    def pe(fn, reads, writes):
        return emit(PE, fn, reads, writes)

    def transpose_to(dst_ap, dst_buf, src_ap, src_buf, rows, cols, eng="act"):
        dsts = [dst_buf, b_fst] if dst_buf is b_stin else [dst_buf]
        return _transpose_to(dst_ap, dsts, src_ap, src_buf, rows, cols, eng)

    def _transpose_to(dst_ap, dst_bufs, src_ap, src_buf, rows, cols, eng="act"):
        pt, bpt = next_tr()
        srcs = list(src_buf) if isinstance(src_buf, (list, tuple)) else [src_buf]
        pe(lambda e: e.transpose(pt[0:cols, 0:rows], src_ap, ident[0:rows, 0:rows]), srcs + [b_cst], [bpt])
        if eng == "act":
            act(lambda e: e.activation(out=dst_ap, in_=pt[0:cols, 0:rows], func=AF.Copy), [bpt], dst_bufs)
        else:
            dve(lambda e: e.tensor_copy(out=dst_ap, in_=pt[0:cols, 0:rows]), [bpt], dst_bufs)

    def next_wsl():
        i = cnt["wsl"] % NWS
        cnt["wsl"] += 1
        return wsl[i], b_wsl[i]

    def next_stg():
        i = cnt["stg"] % NSTG
        cnt["stg"] += 1
        return stg[i], b_stg[i]

    def load_unit(src_ap, np_, shape_str, **dims):
        st, sbf = next_stg()
        nel = 1
        for d_ in src_ap.shape[1:]:
            nel *= d_
        stv = st[0:np_, 0:nel].rearrange(shape_str, **dims)
        dma(stv, src_ap, sbf, writes=[sbf])
        wu, bwu = next_wsl()
        wuv = wu[0:np_, 0:nel].rearrange(shape_str, **dims)
        cast(wuv, stv, [sbf], [bwu])
        return wuv, bwu

    def stream(loaders, consume, PF=5):
        hd = {}
        nU = len(loaders)
        for i in range(nU + PF):
            if i < nU:
                hd[i] = loaders[i]()
            if i >= PF:
                consume(i - PF, hd.pop(i - PF))

    b_c = bf("consts")
    dma(cst_t[:], cst, b_cst, writes=[b_cst])
    dve(lambda e: e.memset(ones_f[:], 1.0), [], [b_c])
    dve(lambda e: e.tensor_copy(out=ones_r[:], in_=ones_f[:]), [b_c], [b_c])
    dve(lambda e: e.memset(ones_b[:], 1.0), [], [b_c])
    dve(lambda e: e.memset(carry_h[:], 0.0), [], [bf("carry")])
    ev_c = dve(lambda e: e.memset(carry_c[:], 0.0), [], [bf("carry")])
    for j_ in range(2):
        for h_ in range(NH):
            bf("carry_%d_%d" % (j_, h_)).w = ev_c

    b_scT = bf("scT"); b_ada = bf("ada"); b_modT = bf("modT"); b_ngT = bf("ngT")
    dma(c17_t, c17, b_small, writes=[b_small])
    act(lambda e: e.activation(out=c17_t, in_=c17_t, func=AF.Silu), [b_small], [b_small])
    for k in range(8):
        transpose_to(scT[:, k, :], b_scT, c17_t[0:17, k * 128:(k + 1) * 128], b_small, 17, 128)
    dma(ng_t, norm_g, b_ngt, writes=[b_ngt])
    for k in range(8):
        transpose_to(ngT[:, k, :], b_ngT, ng_t[0:16, k * 128:(k + 1) * 128], b_ngt, 16, 128)
    brs = [(xtok[0:1, 256 + 128 * i_:384 + 128 * i_], new_dbuf("brow%d" % i_)) for i_ in range(4)]
    ada_xt = [bf("modT0"), bf("modT1")] + [b_ for _, b_ in brs]
    inherit(ada_xt, [b_xtok])

    def ada_unit(l, ct):
        awv = ada_w[l].rearrange("(k p) n -> p k n", p=128)
        st, sbf = next_stg()
        stv = st[:, :].rearrange("p (k n) -> p k n", k=8)
        dma(stv, awv[:, :, ct * 128:(ct + 1) * 128], sbf, writes=[sbf])
        brow_, b_brow_ = brs[ct % 4]
        dma(brow_, ada_b[l:l + 1, ct * 128:(ct + 1) * 128], b_brow_, writes=[b_brow_])
        pm, bpm = next_tr()

        def mm_ada(e):
            for k in range(8):
                e.matmul(pm[0:17, 0:128], scT[:, k, :], stv[:, k, :], start=(k == 0), stop=False)
            return e.matmul(pm[0:17, 0:128], ones_f[0:1, 0:17], brow_, start=False, stop=True)
        pe(mm_ada, [sbf, b_brow_, b_scT, b_c], [bpm])
        mt = modT[:, (ct % 2) * 128:(ct % 2 + 1) * 128]
        bmt = bf("modT%d" % (ct % 2))
        act(lambda e: e.activation(out=mt, in_=pm[0:17, 0:128], func=AF.Copy), [bpm], [bmt])
        transpose_to(ada_fm[:, l, ct, :], bf("ada%d" % l), mt, bmt, 17, 128, eng="dve")

    def ada_finish(l):
        b_al = bf("ada%d" % l)
        for blk, gi_, plus1 in ((1, 0, True), (2, 1, False), (4, 2, True), (5, 3, False)):
            a_ = ada_fm[:, l, blk * 8:(blk + 1) * 8, :]
            g_ = ngT[:, :, l * 4 + gi_:l * 4 + gi_ + 1].to_broadcast([128, 8, 17])
            if plus1:
                dve(lambda e, a_=a_, g_=g_: e.scalar_tensor_tensor(out=a_, in0=a_, scalar=1.0, in1=g_, op0=ALU.add,
                                                                  op1=ALU.mult), [b_al, b_ngT], [b_al])
            else:
                dve(lambda e, a_=a_, g_=g_: e.tensor_tensor(out=a_, in0=a_, in1=g_, op=ALU.mult), [b_al, b_ngT], [b_al])

    for ct in range(48):
        ada_unit(0, ct)
    ada_finish(0)
    ada_pending = {"l": 1, "ct": 0}

    def ada_spread(k):
        for _ in range(k):
            l_ = ada_pending["l"]
            if l_ > 3:
                return
            ada_unit(l_, ada_pending["ct"])
            ada_pending["ct"] += 1
            if ada_pending["ct"] == 48:
                ada_finish(l_)
                ada_pending["l"] += 1
                ada_pending["ct"] = 0

    def modp(l, blk, k):
        return ada_fm[:, l, blk * 8 + k, 0:1]

    def mods(l, blk, k):
        return ada_fm[:, l, blk * 8 + k, 1:17].unsqueeze(1).to_broadcast([128, TS, NS])

    def v3(ap):
        return ap.rearrange("p (t s) -> p t s", t=TS)

    class Tile:
        pass

    def stats_rstd(src_fn, src_bufs, n):
        pst, bst = ps[5], b_ps[5]
        for k in range(8):
            i = cnt["sq"] % 3
            cnt["sq"] += 1
            sq, bsq = sqt[i], bf("sqt%d" % i)
            if k % 3 == 1:
                dve(lambda e, sq=sq, k=k: e.tensor_tensor(out=sq[:, 0:n], in0=src_fn(k), in1=src_fn(k), op=ALU.mult), src_bufs, [bsq])
            else:
                act(lambda e, sq=sq, k=k: e.activation(out=sq[:, 0:n], in_=src_fn(k), func=AF.Square), src_bufs, [bsq])
            pe(lambda e, sq=sq, k=k: e.matmul(pst[:, 0:n], ones_r[:], sq[:, 0:n], start=(k == 0), stop=(k == 7)),
               [bsq, b_c], [bst])
        b_rs = bf("rstd")
        act(lambda e: e.activation(out=rstd[:, 0:n], in_=pst[:, 0:n], func=AF.Ln, scale=1.0 / D, bias=EPS), [bst], [b_rs])
        act(lambda e: e.activation(out=rstd[:, 0:n], in_=rstd[:, 0:n], func=AF.Exp, scale=-0.5), [b_rs], [b_rs])
        return b_rs

    def pre_norm(l, tl, blk_g, blk_sh, b_dst):
        n, c0 = tl.n, tl.c0
        b_rs = stats_rstd(lambda k: xg[:, k, c0:c0 + n], [tl.bx], n)
        for k in range(8):
            i = cnt["ta"] % 2
            cnt["ta"] += 1
            ta, bta = tmpa[i], bf("tmpa%d" % i)
            xin = xg[:, k, c0:c0 + n]
            dst = hin[:, k, c0:c0 + n]
            if tl.kind == "p":
                dve(lambda e, ta=ta, xin=xin, k=k: e.scalar_tensor_tensor(out=ta[:, 0:n], in0=xin, scalar=modp(l, blk_g, k),
                                                                      in1=rstd[:, 0:n], op0=ALU.mult, op1=ALU.mult),
                    [tl.bx, b_rs, bf("ada%d" % l)], [bta])
                act(lambda e, ta=ta, dst=dst, k=k: e.activation(out=dst, in_=ta[:, 0:n], func=AF.Identity,
                                                              bias=modp(l, blk_sh, k), scale=1.0), [bta, bf("ada%d" % l)], b_dst)
            else:
                dve(lambda e, ta=ta, xin=xin: e.tensor_tensor(out=ta[:, 0:n], in0=xin, in1=rstd[:, 0:n], op=ALU.mult),
                    [tl.bx, b_rs], [bta])
                dve(lambda e, ta=ta, k=k: e.tensor_tensor(out=v3(ta[:, 0:n]), in0=v3(ta[:, 0:n]), in1=mods(l, blk_g, k),
                                                        op=ALU.mult), [bta, bf("ada%d" % l)], [bta])
                dve(lambda e, ta=ta, dst=dst, k=k: e.tensor_tensor(out=v3(dst), in0=v3(ta[:, 0:n]), in1=mods(l, blk_sh, k),
                                                                 op=ALU.add), [bta, bf("ada%d" % l)], b_dst)

    def post_norm_residual(l, tl, src_fn, src_bufs, blk):
        n, c0 = tl.n, tl.c0
        b_rs = stats_rstd(src_fn, src_bufs, n)
        for k in range(8):
            i = cnt["ta"] % 2
            cnt["ta"] += 1
            ta, bta = tmpa[i], bf("tmpa%d" % i)
            xin = xg[:, k, c0:c0 + n]
            if tl.kind == "p":
                dve(lambda e, ta=ta, k=k: e.scalar_tensor_tensor(out=ta[:, 0:n], in0=src_fn(k), scalar=modp(l, blk, k),
                                                              in1=rstd[:, 0:n], op0=ALU.mult, op1=ALU.mult),
                    list(src_bufs) + [b_rs, bf("ada%d" % l)], [bta])
            else:
                dve(lambda e, ta=ta, k=k: e.tensor_tensor(out=ta[:, 0:n], in0=src_fn(k), in1=rstd[:, 0:n], op=ALU.mult),
                    list(src_bufs) + [b_rs], [bta])
                dve(lambda e, ta=ta, k=k: e.tensor_tensor(out=v3(ta[:, 0:n]), in0=v3(ta[:, 0:n]), in1=mods(l, blk, k),
                                                        op=ALU.mult), [bta, bf("ada%d" % l)], [bta])
            dve(lambda e, ta=ta, xin=xin: e.tensor_tensor(out=xin, in0=xin, in1=ta[:, 0:n], op=ALU.add), [bta, tl.bx], [tl.bx])

    b_lvec = bf("lvec"); b_gw = bf("gw"); b_lvin = new_dbuf("lvin"); b_gwst = new_dbuf("gwst")

    def prep_lru(j):
        vecs = [lru_conv_w[j, 0], lru_conv_w[j, 1], lru_conv_w[j, 2], lru_conv_w[j, 3], lru_conv_b[j], lru_ba[j],
                lru_bx[j], lru_lam[j]]
        for vi, v in enumerate(vecs):
            dma(lvec_in[vi * 16:(vi + 1) * 16, :], v.rearrange("(h c) -> h c", c=HB), b_lvin, writes=[b_lvin])
        transpose_to(lvec[:, :], b_lvec, lvec_in[:, :], b_lvin, 128, HB)
        lam_ = lvec[:, 112:128]
        act(lambda e: e.activation(out=lam_, in_=lam_, func=AF.Exp, scale=-1.0), [b_lvec], [b_lvec])
        act(lambda e: e.activation(out=lam_, in_=lam_, func=AF.Ln, bias=1.0, scale=1.0), [b_lvec], [b_lvec])
        act(lambda e: e.mul(out=lam_, in_=lam_, mul=-4.0), [b_lvec], [b_lvec])
        act(lambda e: e.mul(out=lvec[:, 80:112], in_=lvec[:, 80:112], mul=0.5), [b_lvec], [b_lvec])
        for gi_, wsrc in enumerate((lru_wa, lru_wx)):
            for hh_ in range(2):
                st, sbf = next_stg()
                stv = st[0:HB, 0:8 * HB].rearrange("p (h j) -> p h j", h=8)
                dma(stv, wsrc[j, hh_ * 8:(hh_ + 1) * 8].rearrange("h i j -> i h j"), sbf, writes=[sbf])
                dve(lambda e, gi_=gi_, stv=stv, hh_=hh_: e.tensor_copy(out=gw[:, gi_, hh_ * 8:(hh_ + 1) * 8, :], in_=stv),
                    [sbf], [b_gw])

    def lv(vi, h):
        return lvec[:, vi * 16 + h:vi * 16 + h + 1]

    b_sgp = bf("sgp"); b_sgin = new_dbuf("sgin"); b_wsst = new_dbuf("wsst"); b_wt = bf("wt"); b_lnbc = new_dbuf("lnbc"); b_e4 = new_dbuf("e4")

    def prep_sgu(j, need_sample):
        dma(sgv_in[:, :], sg_b_in[j, 0:DR].rearrange("(g c) -> g c", c=HB), b_sgin, writes=[b_sgin])
        transpose_to(sgbu[:, :], b_sgp, sgv_in[:, :], b_sgin, 16, HB)
        for hh_ in range(2):
            st, sbf = next_stg()
            bvrow_f = st[0:1, 0:768]
            dma(bvrow_f, sg_b_in[j:j + 1, DR + hh_ * 768:DR + (hh_ + 1) * 768], sbf, writes=[sbf])
            dve(lambda e, hh_=hh_, bvrow_f=bvrow_f: e.tensor_copy(out=bvrow[0:1, hh_ * 768:(hh_ + 1) * 768], in_=bvrow_f),
                [sbf], [b_sgp])
            st, sbf = next_stg()
            bsr_f = st[0:1, 0:8 * 128].rearrange("p (g t) -> p g t", g=8)
            dma(bsr_f, sg_bs[j:j + 1, hh_ * 8:(hh_ + 1) * 8], sbf, writes=[sbf])
            dve(lambda e, hh_=hh_, bsr_f=bsr_f: e.tensor_copy(out=bsr[0:1, hh_ * 8:(hh_ + 1) * 8, :], in_=bsr_f), [sbf], [b_sgp])
            if need_sample:
                dve(lambda e, hh_=hh_, bsr_f=bsr_f: e.tensor_copy(
                    out=bss[0:1, hh_ * 8:(hh_ + 1) * 8],
                    in_=bsr_f[0:1, :, 0:TS].unsqueeze(3).to_broadcast([1, 8, TS, NS])), [sbf], [b_sgp])
        for q in range(4):
            st_, sbf_ = next_stg()
            wsq = st_[:, 0:512].rearrange("p (g s) -> p g s", g=4)
            dma(wsq, sg_ws[j, q * 4:(q + 1) * 4].rearrange("g t s -> t g s"), sbf_, writes=[sbf_])
            dve(lambda e, wsq=wsq: e.tensor_tensor(out=wsq, in0=wsq, in1=trilm.unsqueeze(1).to_broadcast([128, 4, 128]),
                                                  op=ALU.mult), [sbf_, b_cst], [sbf_])
            pt, bpt = next_tr()

            def trs(e, pt=pt, wsq=wsq):
                for gg in range(4):
                    ins = e.transpose(pt[:, gg * 128:(gg + 1) * 128], wsq[:, gg, :], ident[:, :])
                return ins
            pe(trs, [sbf_, b_cst], [bpt])
            act(lambda e, pt=pt, q=q: e.activation(out=wtg[:, q * 4:(q + 1) * 4, :],
                                                  in_=pt[:, :].rearrange("p (g t) -> p g t", g=4), func=AF.Copy),
                [bpt], [b_wt])
        if need_sample:
            for t_ in range(TS):
                dma(e4[:, :, t_], sg_ws[j, :, t_, 0:TS].rearrange("g u -> u g"), b_e4, writes=[b_e4],
                    allow_slow_non_contiguous=True)
            dve(lambda e: e.tensor_copy(out=x4[:], in_=e4[:, :, :].unsqueeze(3).to_broadcast([4, NH, TS, NS])),
                [b_e4], [b_sgp])
            for hf_ in range(2):
                pm, bpm = next_mm()
                pe(lambda e, pm=pm, hf_=hf_: e.matmul(pm[0:64, :], Pm, x4[:, hf_ * 8:(hf_ + 1) * 8].rearrange(
                    "p g t s -> p (g t s)"), start=True, stop=True), [b_sgp, b_cst], [bpm])
                dve(lambda e, pm=pm, hf_=hf_: e.tensor_tensor(
                    out=wts[:, hf_ * 8:(hf_ + 1) * 8, :], in0=pm[0:64, :].rearrange("p (g n) -> p g n", g=8),
                    in1=Dm.unsqueeze(1).to_broadcast([64, 8, 64]), op=ALU.mult), [bpm, b_cst], [b_wt])
        dma(lngb, sg_ln_g[j].partition_broadcast(128), b_lnbc, writes=[b_lnbc, bf("scr_hi")])
        dma(lnbb, sg_ln_b[j].partition_broadcast(128), b_lnbc, writes=[b_lnbc, bf("scr_hi")])

    b_y = bf("ybuf"); b_wres = bf("wres"); b_wvs = [bf("wv%d" % i_) for i_ in range(3)]; b_hin = bf("hin"); b_lt = bf("ltmp"); b_carry = bf("carry")
    b_hsm = bf("hsm"); b_csm = bf("csm")

    prep_done = {}

    def lru_mixer(l, j, tiles, first_group, has_sample):
        if not prep_done.get(("lru", l)):
            prep_lru(j)
        if has_sample:
            dma(st_in[0:NS, :], h0s[j], b_stin, writes=[b_stin, b_fst])
            for h in range(NH):
                transpose_to(hsm[:, j, h, :], bf("hsm_%d_%d" % (j, h)), st_in[0:NS, h * HB:(h + 1) * HB], b_stin, NS, HB, eng="dve")
            dma(st_in[0:3 * NS, :], cvs[j], b_stin, writes=[b_stin, b_fst])
            for h in range(NH):
                transpose_to(csm[:, j, h, :], bf("csm_%d_%d" % (j, h)), st_in[0:3 * NS, h * HB:(h + 1) * HB], b_stin, 3 * NS, HB, eng="dve")
        win = lru_w_in[j].rearrange("(k p) n -> p k n", p=128)
        lt_all = [bf("lt%d_%s" % (ss, nm)) for ss in range(3) for nm in ("xpb", "xc", "r", "ig", "a", "hs", "gl", "xcb")]
        inherit(lt_all, [b_vbf, b_vg, b_lnbc, bf("scr_hi"), b_ngt, b_wt, b_sgp])
        items = [(h, tl) for h in range(NH) for tl in tiles]
        ctx = {}

        def load_head(hh_):
            ctx[hh_] = [load_unit(win[:, :, (DR if br_ == 0 else 0) + hh_ * HB:(DR if br_ == 0 else 0) + (hh_ + 1) * HB], 128,
                                  "p (k n) -> p k n", k=8) for br_ in range(2)]
        cast_cfg["engs"] = [ACT, POOL, ACT]
        load_head(0)
        load_head(1)

        def stageA(idx):
            h, tl = items[idx]
            n, c0 = tl.n, tl.c0
            ns = 1 if tl.kind == "p" else NS
            hl = 3 * ns
            sset = idx % 3
            REG, base = (SCR, sset * 4224) if sset < 2 else (SG2, 0)
            T = dict(h=h, tl=tl, n=n, ns=ns, hl=hl)
            T["xpb"] = xpb = REG[0:HB, base:base + 520]
            T["xc"], T["r"], T["ig"], T["a"], T["hs"], T["gl"] = [REG[0:HB, base + 520 + q * 512:base + 520 + (q + 1) * 512]
                                                                  for q in range(6)]
            T["xcb"] = xcb = REG[0:HB, base + 3592:base + 3848].bitcast(BF16)
            for nm in ("xpb", "xc", "r", "ig", "a", "hs", "gl", "xcb"):
                T["b" + nm] = bf("lt%d_%s" % (sset, nm))
            xc, gl = T["xc"], T["gl"]
            T["b_ch"] = b_ch = bf("carry_%d_%d" % (j, h)); T["b_hs_"] = bf("hsm_%d_%d" % (j, h))
            T["b_cs_"] = b_cs_ = bf("csm_%d_%d" % (j, h))
            if tl is tiles[0] and h + 2 < NH:
                load_head(h + 2)
            (wx_, bwx_), (wg_, bwg_) = ctx[h]
            pmx, bpx = next_mm()
            pe(lambda e: [e.matmul(pmx[0:HB, 0:n], wx_[:, k, :], hin[:, k, c0:c0 + n], start=(k == 0), stop=(k == 7))
                          for k in range(8)][-1], [bwx_, b_hin], [bpx])
            pmg, bpg = next_mm()
            pe(lambda e: [e.matmul(pmg[0:HB, 0:n], wg_[:, k, :], hin[:, k, c0:c0 + n], start=(k == 0), stop=(k == 7))
                          for k in range(8)][-1], [bwg_, b_hin], [bpg])
            if tl.kind == "p":
                dve(lambda e: e.tensor_copy(out=xpb[:, 0:3], in_=carry_c[:, j, :, h]), [b_ch], [T["bxpb"]])
            else:
                dve(lambda e: e.tensor_copy(out=xpb[:, 0:hl], in_=csm[:, j, h, :]), [b_cs_], [T["bxpb"]])
            dve(lambda e: e.tensor_copy(out=xpb[:, hl:hl + n], in_=pmx[0:HB, 0:n]), [bpx], [T["bxpb"]])
            act(lambda e: e.activation(out=gl[:, 0:n], in_=pmg[0:HB, 0:n], func=AF.Gelu_apprx_tanh), [bpg], [T["bgl"]])
            dve(lambda e: e.tensor_scalar(out=xc[:, 0:n], in0=xpb[:, 0:n], scalar1=lv(0, h), scalar2=lv(4, h),
                                          op0=ALU.mult, op1=ALU.add), [T["bxpb"], b_lvec], [T["bxc"]])
            for kk in range(1, 4):
                dve(lambda e, kk=kk: e.scalar_tensor_tensor(out=xc[:, 0:n], in0=xpb[:, kk * ns:kk * ns + n], scalar=lv(kk, h),
                                                            in1=xc[:, 0:n], op0=ALU.mult, op1=ALU.add),
                    [T["bxpb"], b_lvec], [T["bxc"]])
            if tl.kind == "p":
                dve(lambda e: e.tensor_copy(out=carry_c[:, j, :, h], in_=xpb[:, n:n + 3]), [T["bxpb"]], [b_ch])
            else:
                dve(lambda e: e.tensor_copy(out=csm[:, j, h, :], in_=xpb[:, n:n + hl]), [T["bxpb"]], [b_cs_])
            pool(lambda e: e.tensor_copy(out=xcb[:, 0:n], in_=xc[:, 0:n]), [T["bxc"]], [T["bxcb"]])
            return T

        def stageB(T):
            h, tl, n = T["h"], T["tl"], T["n"]
            c0 = tl.c0
            xc, r_, ig, a_, hs_, gl = T["xc"], T["r"], T["ig"], T["a"], T["hs"], T["gl"]
            bxc, br, big_, ba_, bhs, bgl = T["bxc"], T["br"], T["big"], T["ba"], T["bhs"], T["bgl"]
            b_ch, b_hs_ = T["b_ch"], T["b_hs_"]
            xcb = T["xcb"]
            pma, bpa = next_mm()
            pe(lambda e: e.matmul(pma[0:HB, 0:n], gw[:, 0, h, :], xcb[:, 0:n], start=True, stop=True), [b_gw, T["bxcb"]], [bpa])
            pmi, bpi = next_mm()
            pe(lambda e: e.matmul(pmi[0:HB, 0:n], gw[:, 1, h, :], xcb[:, 0:n], start=True, stop=True), [b_gw, T["bxcb"]], [bpi])
            act(lambda e: e.activation(out=r_[:, 0:n], in_=pma[0:HB, 0:n], func=AF.Tanh, bias=lv(5, h), scale=0.5),
                [bpa, b_lvec], [br])
            act(lambda e: e.activation(out=ig[:, 0:n], in_=pmi[0:HB, 0:n], func=AF.Tanh, bias=lv(6, h), scale=0.5),
                [bpi, b_lvec], [big_])
            act(lambda e: e.activation(out=a_[:, 0:n], in_=r_[:, 0:n], func=AF.Exp, scale=lv(7, h), bias=lv(7, h)),
                [br, b_lvec], [ba_])
            act(lambda e: e.activation(out=r_[:, 0:n], in_=a_[:, 0:n], func=AF.Square), [ba_], [br])
            act(lambda e: e.activation(out=r_[:, 0:n], in_=r_[:, 0:n], func=AF.Ln, scale=-1.0, bias=1.0), [br], [br])
            act(lambda e: e.activation(out=r_[:, 0:n], in_=r_[:, 0:n], func=AF.Exp, scale=0.5, bias=LN_HALF), [br], [br])
            dve(lambda e: e.tensor_tensor(out=r_[:, 0:n], in0=r_[:, 0:n], in1=xc[:, 0:n], op=ALU.mult), [br, bxc], [br])
            dve(lambda e: e.scalar_tensor_tensor(out=ig[:, 0:n], in0=ig[:, 0:n], scalar=1.0, in1=r_[:, 0:n], op0=ALU.add,
                                                 op1=ALU.mult), [big_, br], [big_])
            if tl.kind == "p":
                dve(lambda e: e.tensor_tensor_scan(out=hs_[:, 0:n], data0=a_[:, 0:n], data1=ig[:, 0:n],
                                                   initial=carry_h[:, j, h, 0:1], op0=ALU.mult, op1=ALU.add),
                    [ba_, big_, b_ch], [bhs])
                dve(lambda e: e.tensor_copy(out=carry_h[:, j, h, 0:1], in_=hs_[:, n - 1:n]), [bhs], [b_ch])
            else:
                for t in range(TS):
                    prev = hsm[:, j, h, :] if t == 0 else hs_[:, (t - 1) * NS:t * NS]
                    dve(lambda e, t=t, prev=prev: e.tensor_tensor(out=hs_[:, t * NS:(t + 1) * NS], in0=a_[:, t * NS:(t + 1) * NS],
                                                                in1=prev, op=ALU.mult), [ba_, b_hs_, bhs], [bhs])
                    dve(lambda e, t=t: e.tensor_tensor(out=hs_[:, t * NS:(t + 1) * NS], in0=hs_[:, t * NS:(t + 1) * NS],
                                                     in1=ig[:, t * NS:(t + 1) * NS], op=ALU.add), [bhs, big_], [bhs])
                dve(lambda e: e.tensor_copy(out=hsm[:, j, h, :], in_=hs_[:, (TS - 1) * NS:TS * NS]), [bhs], [b_hs_])
            dve(lambda e: e.tensor_tensor(out=ybuf[0:HB, h, c0:c0 + n], in0=hs_[:, 0:n], in1=gl[:, 0:n], op=ALU.mult),
                [bhs, bgl], [b_y, b_hid])

        DEPTH_ = 2
        Ts = {i_: stageA(i_) for i_ in range(min(DEPTH_, len(items)))}
        for idx in range(len(items)):
            ada_spread(2)
            stageB(Ts.pop(idx))
            if idx + DEPTH_ < len(items):
                Ts[idx + DEPTH_] = stageA(idx + DEPTH_)
        cast_cfg["engs"] = [ACT, DVE]

    b_vbf = bf("vbf"); b_vg = bf("vg"); b_ln = bf("ln")

    def sgu_mixer(l, j, tiles, has_sample):
        lt_all = [bf("lt%d_%s" % (ss, nm)) for ss in range(3) for nm in ("xpb", "xc", "r", "ig", "a", "hs", "gl", "xcb")]
        if not prep_done.get(("sgu", l)):
            inherit([b_vbf, b_vg, b_lnbc, b_wt, b_sgp], lt_all)
            prep_sgu(j, has_sample)
        win = sg_w_in[j].rearrange("(k p) n -> p k n", p=128)
        for q in range(12):
            st, sbf = next_stg()
            stv = st[:, :].rearrange("p (k n) -> p k n", k=8)
            dma(stv, win[:, :, DR + q * 128:DR + (q + 1) * 128], sbf, writes=[sbf])
            cast(wv[:, :, q * 128:(q + 1) * 128], stv, [sbf], [b_wvs[q // 4], b_hid, b_y])
        chunks = []
        for tl in tiles:
            if tl.kind == "p":
                for cc in range(tl.n // 128):
                    chunks.append((tl, tl.c0 + cc * 128, 128, cc))
            else:
                chunks.append((tl, tl.c0, NSMP, 0))
        for ci, (tl, cc0, cn, _) in enumerate(chunks):
            for ct in range(3):
                pm, bpm = next_mm()

                def mmv(e, pm=pm, ct=ct):
                    for k in range(8):
                        e.matmul(pm[0:cn, :], hin[:, k, cc0:cc0 + cn], wv[:, k, ct * 512:(ct + 1) * 512], start=(k == 0),
                                 stop=False)
                    return e.matmul(pm[0:cn, :], ones_b[0:1, 0:cn], bvrow[0:1, ct * 512:(ct + 1) * 512], start=False,
                                    stop=True)
                pe(mmv, [b_hin, b_wvs[ct], b_sgp, b_c], [bpm])
                act(lambda e, pm=pm, ct=ct: e.activation(out=vg[0:cn, ct * 512:(ct + 1) * 512], in_=pm[0:cn, :],
                                                        func=AF.Gelu_apprx_tanh), [bpm], [b_vg])
                dve(lambda e, ct=ct: e.bn_stats(out=lnst[0:cn, ct, :], in_=vg[0:cn, ct * 512:(ct + 1) * 512]), [b_vg], [b_ln])
            dve(lambda e: e.bn_aggr(out=lnag[0:cn, :], in_=lnst[0:cn, :, :].rearrange("p a b -> p (a b)")), [b_ln], [b_ln])
            act(lambda e: e.activation(out=lnr[0:cn, 0:1], in_=lnag[0:cn, 1:2], func=AF.Ln, bias=EPS, scale=1.0), [b_ln], [b_ln])
            act(lambda e: e.activation(out=lnr[0:cn, 1:2], in_=lnr[0:cn, 0:1], func=AF.Exp, scale=-0.5), [b_ln], [b_ln])
            dve(lambda e: e.scalar_tensor_tensor(out=vg[0:cn, :], in0=vg[0:cn, :], scalar=lnag[0:cn, 0:1], in1=lngb[0:cn, :],
                                                 op0=ALU.subtract, op1=ALU.mult), [b_vg, b_ln, b_lnbc], [b_vg])
            if tl.kind == "p":
                dve(lambda e, ci=ci: e.scalar_tensor_tensor(out=vbf[0:cn, ci, :], in0=vg[0:cn, :], scalar=lnr[0:cn, 1:2],
                                                          in1=lnbb[0:cn, :], op0=ALU.mult, op1=ALU.add),
                    [b_vg, b_ln, b_lnbc], [b_vbf])
            else:
                dve(lambda e: e.scalar_tensor_tensor(out=vg[0:cn, :], in0=vg[0:cn, :], scalar=lnr[0:cn, 1:2],
                                                     in1=lnbb[0:cn, :], op0=ALU.mult, op1=ALU.add),
                    [b_vg, b_ln, b_lnbc], [b_vg])
                act(lambda e, ci=ci: e.activation(out=vbf[0:cn, ci, :], in_=vg[0:cn, :], func=AF.Copy), [b_vg], [b_vbf])
                dma(vs_o[j], vg[0:cn, :], b_out, reads=[b_vg])
        def u_consume(g, hd_):
            wuv, bwu = hd_
            for tl in tiles:
                n, c0 = tl.n, tl.c0
                pmu, bpu = next_mm()
                pe(lambda e, pmu=pmu, wuv=wuv: [e.matmul(pmu[0:HB, 0:n], wuv[:, k, :], hin[:, k, c0:c0 + n], start=(k == 0),
                                                        stop=(k == 7)) for k in range(8)][-1], [bwu, b_hin], [bpu])
                i = cnt["ta"] % 2
                cnt["ta"] += 1
                u_, bu_ = tmpa[i][0:HB, :], bf("tmpa%d" % i)
                act(lambda e, pmu=pmu, u_=u_: e.activation(out=u_[:, 0:n], in_=pmu[0:HB, 0:n], func=AF.Gelu_apprx_tanh,
                                                          bias=sgbu[:, g:g + 1], scale=1.0), [bpu, b_sgp], [bu_])
                pmm, bpm = next_mm()

                def mmix(e, pmm=pmm):
                    ins = None
                    for ci, (ctl, cc0, cn, cc) in enumerate(chunks):
                        if ctl is not tl:
                            continue
                        off = cc0 - c0
                        if ctl.kind == "p":
                            e.matmul(pmm[0:HB, off:off + cn], vbf[0:cn, ci, g * HB:(g + 1) * HB], wtg[:, g, :], start=True,
                                     stop=False)
                            ins = e.matmul(pmm[0:HB, off:off + cn], ones_b[0:1, 0:HB], bsr[0:1, g, :], start=False, stop=True)
                        else:
                            e.matmul(pmm[0:HB, off:off + cn], vbf[0:cn, ci, g * HB:(g + 1) * HB], wts[:, g, :], start=True,
                                     stop=False)
                            ins = e.matmul(pmm[0:HB, off:off + cn], ones_b[0:1, 0:HB],
                                           bss[0:1, g].rearrange("p t s -> p (t s)"), start=False, stop=True)
                    return ins
                pe(mmix, [b_vbf, b_wt, b_sgp, b_c], [bpm])
                dve(lambda e, pmm=pmm, u_=u_: e.tensor_tensor(out=ybuf[0:HB, g, c0:c0 + n], in0=u_[:, 0:n],
                                                            in1=pmm[0:HB, 0:n], op=ALU.mult), [bpm, bu_], [b_y, b_hid])

        stream([(lambda g=g: load_unit(win[:, :, g * HB:(g + 1) * HB], 128, "p (k n) -> p k n", k=8)) for g in range(NH)],
               u_consume, PF=5)


    def mlp(l, tiles):
        wup = mlp_w_up[l].rearrange("(k p) n -> p k n", p=128)
        wdn = mlp_w_down[l].rearrange("(h p) n -> p h n", p=128)
        def up_consume(hc, hd_):
            wuv, bwu = hd_
            ada_spread(1)
            for tl in tiles:
                n, c0 = tl.n, tl.c0
                pm, bpm = next_mm()
                pe(lambda e, pm=pm, n=n, c0=c0: [e.matmul(pm[:, 0:n], wuv[:, k, :], hin[:, k, c0:c0 + n],
                                                         start=(k == 0), stop=(k == 7)) for k in range(8)][-1],
                   [bwu, b_hin], [bpm])
                i = cnt["ta"] % 2
                cnt["ta"] += 1
                ta, bta = tmpa[i], bf("tmpa%d" % i)
                act(lambda e, pm=pm, ta=ta, n=n: e.activation(out=ta[:, 0:n], in_=pm[:, 0:n], func=AF.Relu), [bpm], [bta])
                E = dve
                E(lambda e, ta=ta, n=n, c0=c0: e.tensor_tensor(out=hidden[:, hc, c0:c0 + n], in0=ta[:, 0:n],
                                                             in1=ta[:, 0:n], op=ALU.mult), [bta], [b_hid, b_y] + b_wvs)
        stream([(lambda hc=hc: load_unit(wup[:, :, hc * 128:(hc + 1) * 128], 128, "p (k n) -> p k n", k=8))
                for hc in range(32)], up_consume)
        dstate = {}

        def dn_consume(i, hd_):
            oc, q = i // 4, i % 4
            wdv, bwd = hd_
            if q == 0:
                dstate["pms"] = [next_mm() for _ in tiles]
            pms = dstate["pms"]
            for ti, tl in enumerate(tiles):
                n, c0 = tl.n, tl.c0
                pm, bpm = pms[ti]
                pe(lambda e, pm=pm, n=n, c0=c0: [e.matmul(pm[:, 0:n], wdv[:, hh, :], hidden[:, q * 8 + hh, c0:c0 + n],
                                                         start=(q == 0 and hh == 0), stop=(q == 3 and hh == 7))
                                                for hh in range(8)][-1], [bwd, b_hid], [bpm])
            if q == 3:
                for ti, tl in enumerate(tiles):
                    n, c0 = tl.n, tl.c0
                    pm, bpm = pms[ti]
                    act(lambda e, pm=pm, n=n, c0=c0: e.activation(out=fstore[:, oc, c0:c0 + n], in_=pm[:, 0:n], func=AF.Copy),
                        [bpm], [b_fst, b_hin, b_stin])
        stream([(lambda oc=oc, q=q: load_unit(wdn[:, q * 8:(q + 1) * 8, oc * 128:(oc + 1) * 128], 128, "p (h n) -> p h n", h=8))
                for oc in range(8) for q in range(4)], dn_consume)
        for tl in tiles:
            post_norm_residual(l, tl, lambda k, tl=tl: fstore[:, k, tl.c0:tl.c0 + tl.n], [b_fst], 5)

    last_sample_group = max(gi for gi, g in enumerate(GROUPS) if any(t[0] == "s" for t in g))
    last_prompt_group = max(gi for gi, g in enumerate(GROUPS) if any(t[0] == "p" for t in g))
    for gi, gdef in enumerate(GROUPS):
        tiles = []
        c0 = 0
        for (kind, off) in gdef:
            tl = Tile()
            tl.kind, tl.off, tl.c0 = kind, off, c0
            tl.n = 512 if kind == "p" else NSMP
            tl.bx = bf("xg_%d_%d" % (gi, c0))
            tiles.append(tl)
            c0 += tl.n
        has_sample = any(t.kind == "s" for t in tiles)
        inherit([b_xtok], ada_xt)
        for tl in tiles:
            nblk = tl.n // 128 if tl.kind == "p" else 1
            for bi in range(nblk):
                rows = 128 if tl.kind == "p" else NSMP
                src = xp[tl.off + bi * 128:tl.off + (bi + 1) * 128, :] if tl.kind == "p" else xs[:, :]
                if bi % 2 == 0:
                    halves = [(xtok[0:rows, 0:512], b_xtok), (xtok[0:rows, 512:1024], b_xtok)]
                else:
                    halves = [(tmpa[0][0:rows, :], bf("tmpa0")), (tmpa[1][0:rows, :], bf("tmpa1"))]
                for hv, (hap, hb) in enumerate(halves):
                    dbs = (b_xtok, b_xtok2) if hb is b_xtok else (b_xalt, b_xalt2)
                    dma(hap, src[:, hv * 512:(hv + 1) * 512], dbs[hv], writes=[hb])
                for kq in range(2):
                    pt, bpt = next_tr()
                    hap, hb = halves[kq]

                    def trx(e, pt=pt, kq=kq, rows=rows, hap=hap):
                        for kk in range(4):
                            ins = e.transpose(pt[:, kk * rows:(kk + 1) * rows], hap[:, kk * 128:(kk + 1) * 128],
                                              ident[0:rows, 0:rows])
                        return ins
                    pe(trx, [hb, b_cst], [bpt])
                    cdst = tl.c0 + bi * 128
                    act(lambda e, pt=pt, kq=kq, rows=rows, cdst=cdst: e.activation(
                        out=xg[:, kq * 4:(kq + 1) * 4, cdst:cdst + rows],
                        in_=pt[:, 0:4 * rows].rearrange("p (k n) -> p k n", k=4), func=AF.Copy), [bpt], [tl.bx])
        inherit(ada_xt, [b_xtok])
        for l in range(4):
            j = l // 2
            while ada_pending["l"] <= l:
                ada_spread(1)
            for tl in tiles:
                pre_norm(l, tl, 1, 0, [b_hin, b_fst])
            if l % 2 == 0:
                lru_mixer(l, j, tiles, gi == 0, has_sample)
                lt_all_ = [bf("lt%d_%s" % (ss, nm)) for ss in range(3) for nm in ("xpb", "xc", "r", "ig", "a", "hs", "gl", "xcb")]
                inherit([b_vbf, b_vg, b_lnbc, b_wt, b_sgp], lt_all_)
                prep_sgu(j, has_sample)
                prep_done[("sgu", l + 1)] = True
            else:
                sgu_mixer(l, j, tiles, has_sample)
                if l + 1 < 4:
                    prep_lru(j + 1)
                    prep_done[("lru", l + 1)] = True
            w_out_j = (lru_w_out if l % 2 == 0 else sg_w_out)[j]
            pstate = {}

            def p3_consume(i, hd_, pstate=pstate):
                oc, q = i // 2, i % 2
                wov, bwo = hd_
                if q == 0:
                    pstate["pms"] = [next_mm() for _ in tiles]
                pms = pstate["pms"]
                for ti, tl in enumerate(tiles):
                    n, c0_ = tl.n, tl.c0
                    pm, bpm = pms[ti]
                    pe(lambda e, pm=pm, c0_=c0_, n=n: [e.matmul(pm[:, 0:n], wov[:, hh, :], ybuf[0:HB, q * 8 + hh, c0_:c0_ + n],
                                                               start=(q == 0 and hh == 0), stop=(q == 1 and hh == 7))
                                                      for hh in range(8)][-1], [bwo, b_y], [bpm])
                if q == 1:
                    for ti, tl in enumerate(tiles):
                        n, c0_ = tl.n, tl.c0
                        pm, bpm = pms[ti]
                        act(lambda e, pm=pm, c0_=c0_, n=n: e.activation(out=fstore[:, oc, c0_:c0_ + n], in_=pm[:, 0:n],
                                                                      func=AF.Copy), [bpm], [b_fst, b_hin, b_stin])
            stream([(lambda oc=oc, q=q: load_unit(
                w_out_j[q * 8 * HB:(q + 1) * 8 * HB, oc * 128:(oc + 1) * 128].rearrange("(h c) n -> c h n", c=HB), HB,
                "p (h n) -> p h n", h=8)) for oc in range(8) for q in range(2)], p3_consume)
            for tl in tiles:
                post_norm_residual(l, tl, lambda k, tl=tl: fstore[:, k, tl.c0:tl.c0 + tl.n], [b_fst], 2)
            for tl in tiles:
                pre_norm(l, tl, 4, 3, [b_hin, b_fst])
            mlp(l, tiles)
        inherit([b_xtok], ada_xt)
        for tl in tiles:
            nblk = tl.n // 128 if tl.kind == "p" else 1
            for bi in range(nblk):
                rows = 128 if tl.kind == "p" else NSMP
                csrc = tl.c0 + bi * 128
                for kq in range(2):
                    pt, bpt = next_tr()

                    def try_(e, pt=pt, kq=kq, rows=rows, csrc=csrc):
                        for kk in range(4):
                            k = kq * 4 + kk
                            ins = e.transpose(pt[0:rows, kk * 128:(kk + 1) * 128], xg[:, k, csrc:csrc + rows], ident[:, :])
                        return ins
                    pe(try_, [tl.bx, b_cst], [bpt])
                    act(lambda e, pt=pt, kq=kq, rows=rows: e.activation(out=xtok[0:rows, kq * 512:(kq + 1) * 512],
                                                                      in_=pt[0:rows, :], func=AF.Copy), [bpt], [b_xtok])
                dst = yp[tl.off + bi * 128:tl.off + (bi + 1) * 128, :] if tl.kind == "p" else ys[:, :]
                dma(dst, xtok[0:rows, :], b_xtok, reads=[b_xtok])
        if gi == last_prompt_group:
            for j in range(2):
                transpose_to(st_in[0:NH, 0:HB], b_stin, carry_h[:, j, :, 0], [bf("carry_%d_%d" % (j, h_)) for h_ in range(NH)], HB, NH)
                dma(hp_o[j].rearrange("(h c) -> h c", c=HB), st_in[0:NH, 0:HB], b_stin, reads=[b_stin])
                transpose_to(st_in[0:3 * NH, 0:HB], b_stin, carry_c[:, j].rearrange("p k h -> p (k h)"),
                             [bf("carry_%d_%d" % (j, h_)) for h_ in range(NH)], HB, 3 * NH)
                for kk in range(3):
                    dma(cp_o[j, kk].rearrange("(h c) -> h c", c=HB), st_in[kk * NH:(kk + 1) * NH, 0:HB], b_stin,
                        reads=[b_stin])
        if gi == last_sample_group:
            for j in range(2):
                for h in range(NH):
                    transpose_to(st_in[0:NS, h * HB:(h + 1) * HB], b_stin, hsm[:, j, h, :], bf("hsm_%d_%d" % (j, h)), HB, NS)
                dma(hs_o[j], st_in[0:NS, :], b_stin, reads=[b_stin])
                for h in range(NH):
                    transpose_to(st_in[0:3 * NS, h * HB:(h + 1) * HB], b_stin, csm[:, j, h, :], bf("csm_%d_%d" % (j, h)), HB, 3 * NS)
                dma(cs_o[j], st_in[0:3 * NS, :], b_stin, reads=[b_stin])

    for b in dsems:
        if b.dcnt:
            SP.prog.append(("wait", b.dsem, 16 * b.dcnt))

    def replay(E):
        def body(eng):
            for op in E.prog:
                if op[0] == "wait":
                    eng.wait_ge(op[1], op[2])
                elif op[0] == "dma":
                    eng.dma_start(out=op[1], in_=op[2], **op[3]).then_inc(op[4], 16)
                else:
                    ins = None
                    for (name, a, kw) in op[1]:
                        ins = getattr(eng, name)(*a, **kw)
                    ins.then_inc(E.sem, 1)
        return body

    with nc.Block() as block:
        block.sync(replay(SP))
        block.tensor(replay(PE))
        block.scalar(replay(ACT))
        block.vector(replay(DVE))
        block.gpsimd(replay(POOL))
    es.close()
    return nc


_CACHE = {}


def _consts():
    c = np.zeros((128, 384), np.float32)
    c[:, 0:128] = np.eye(128, dtype=np.float32)
    c[:, 128:256] = np.tril(np.ones((128, 128), np.float32))
    p = np.arange(64)
    t2, s2 = p // NS, p % NS
    c[0:64, 256:320] = ((s2[:, None] == s2[None, :]) & (t2[:, None] <= t2[None, :])).astype(np.float32)
    c[0:4, 320:384] = (np.arange(4)[:, None] == t2[None, :]).astype(np.float32)
    return c


def kernel(**inp):
    if "nc" not in _CACHE:
        _CACHE["nc"] = build_nc()
    nc = _CACHE["nc"]
    f = lambda a: np.ascontiguousarray(np.asarray(a, dtype=np.float32))
    shared = {
        "ada_w": f(inp["ada_w"]), "ada_b": f(inp["ada_b"]), "norm_g": f(inp["norm_g"]).reshape(16, D),
        "lru_w_in": f(inp["lru_w_in"]), "lru_conv_w": f(inp["lru_conv_w"]), "lru_conv_b": f(inp["lru_conv_b"]),
        "lru_wa": f(inp["lru_wa"]), "lru_ba": f(inp["lru_ba"]).reshape(2, DR), "lru_wx": f(inp["lru_wx"]),
        "lru_bx": f(inp["lru_bx"]).reshape(2, DR), "lru_lambda": f(inp["lru_lambda"]), "lru_w_out": f(inp["lru_w_out"]),
        "sg_w_in": f(inp["sg_w_in"]), "sg_b_in": f(inp["sg_b_in"]), "sg_ln_g": f(inp["sg_ln_g"]), "sg_ln_b": f(inp["sg_ln_b"]),
        "sg_ws": f(inp["sg_ws"]), "sg_bs": f(inp["sg_bs"]), "sg_w_out": f(inp["sg_w_out"]),
        "mlp_w_up": f(inp["mlp_w_up"]), "mlp_w_down": f(inp["mlp_w_down"]), "cst": _consts(),
    }
    x_prompt = f(inp["x_prompt"]); x_sample = f(inp["x_sample"]); c_prompt = f(inp["c_prompt"]); c_sample = f(inp["c_sample"])
    sh = f(inp["state_lru_h"]); sc = f(inp["state_lru_conv"])
    in_maps = []
    for c in range(NCORE):
        sl = slice(c * NS, (c + 1) * NS)
        m = dict(shared)
        m["xp"] = x_prompt[c]
        m["xs"] = np.ascontiguousarray(x_sample[sl].transpose(1, 0, 2).reshape(NSMP, D))
        m["c17"] = np.ascontiguousarray(np.concatenate([c_prompt[c:c + 1], c_sample[sl]], 0))
        m["h0s"] = np.ascontiguousarray(sh[:, sl])
        m["cvs"] = np.ascontiguousarray(sc[:, sl].transpose(0, 2, 1, 3).reshape(2, 3 * NS, DR))
        in_maps.append(m)
    res = run_bass_kernel_spmd(nc, in_maps, core_ids=list(range(NCORE)))
    R = res.results
    y_prompt = np.stack([R[c]["yp"] for c in range(NCORE)], 0)
    y_sample = np.concatenate([R[c]["ys"].reshape(TS, NS, D).transpose(1, 0, 2) for c in range(NCORE)], 0)
    h_prompt = np.stack([R[c]["hp"] for c in range(NCORE)], 1)
    conv_prompt = np.stack([R[c]["cp"] for c in range(NCORE)], 1)
    h_sample = np.concatenate([R[c]["hs"] for c in range(NCORE)], 1)
    conv_sample = np.concatenate([R[c]["cs"].reshape(2, 3, NS, DR).transpose(0, 2, 1, 3) for c in range(NCORE)], 1)
    v_sample = np.concatenate([R[c]["vs"].reshape(2, TS, NS, DR).transpose(0, 2, 1, 3) for c in range(NCORE)], 1)
    return tuple(np.ascontiguousarray(a.astype(np.float32)) for a in
                 (y_prompt, y_sample, h_prompt, conv_prompt, h_sample, conv_sample, v_sample))
```
